# Optimizing a Trainium2 kernel written in Bass

```python
import math
import jax, jax.numpy as jnp
from jax import lax
import numpy as np

D_MODEL = 1024
BATCH = 8
SEQ = 4096
DEPTH = 1

CHUNK = 64
Q_BLOCK = 128
SSM_WIDTH = 1024
SSM_GROUP = 16
SSM_GROUPS = SSM_WIDTH // SSM_GROUP
SSM_STATE = 64
DT_MIN = 1e-3
DT_MAX = 1e-1
MLA_HEADS = 16
QK_NOPE = 64
QK_ROPE = 32
V_HEAD = 64
Q_LORA = 256
KV_LORA = 256
MLA_WIDTH = MLA_HEADS * V_HEAD
ROPE_BASE = 10000.0
EPS = 1e-6
IN_SPLITS = (SSM_WIDTH, SSM_WIDTH, Q_LORA, KV_LORA, QK_ROPE, MLA_WIDTH, D_MODEL, D_MODEL)
IN_WIDTH = sum(IN_SPLITS)

kernel_name = "hybrid_s5_mla_gated_block"


def _rmsnorm(x, g):
    xf = x.astype(jnp.float32)
    y = xf * lax.rsqrt(jnp.mean(xf * xf, axis=-1, keepdims=True) + EPS)
    return (y * g.astype(jnp.float32)).astype(x.dtype)


def _cmul(ar, ai, br, bi):
    return ar * br - ai * bi, ar * bi + ai * br


def _ssm_branch(u, log_dt, a_re, a_im, b_re, b_im, c_re, c_im, d_skip):
    f32 = jnp.float32
    out_dtype = u.dtype
    bsz, seq, _ = u.shape
    n_chunks = seq // CHUNK
    u = u.astype(f32).reshape(bsz, n_chunks, CHUNK, SSM_GROUPS, SSM_GROUP)
    u = jnp.moveaxis(u, 1, 0)
    dt = jnp.exp(log_dt.astype(f32))[:, None]
    lr, li = a_re.astype(f32), a_im.astype(f32)
    mag = jnp.exp(lr * dt)
    abar_re, abar_im = mag * jnp.cos(li * dt), mag * jnp.sin(li * dt)
    den = lr * lr + li * li
    nr, ni = abar_re - 1.0, abar_im
    fr = (nr * lr + ni * li) / den
    fi = (ni * lr - nr * li) / den
    br, bi = b_re.astype(f32), b_im.astype(f32)
    bbar_re, bbar_im = _cmul(fr[..., None], fi[..., None], br, bi)
    cr_w, ci_w = c_re.astype(f32), c_im.astype(f32)
    d = d_skip.astype(f32).reshape(SSM_GROUPS, SSM_GROUP)

    def combine(e1, e2):
        a1r, a1i, b1r, b1i = e1
        a2r, a2i, b2r, b2i = e2
        ar, ai = _cmul(a2r, a2i, a1r, a1i)
        xr, xi = _cmul(a2r, a2i, b1r, b1i)
        return ar, ai, xr + b2r, xi + b2i

    def chunk_step(carry, u_c):
        sr, si = carry
        bur = jnp.einsum('btgp,gnp->btgn', u_c, bbar_re)
        bui = jnp.einsum('btgp,gnp->btgn', u_c, bbar_im)
        ar = jnp.broadcast_to(abar_re, bur.shape)
        ai = jnp.broadcast_to(abar_im, bur.shape)
        pr, pi_, xr, xi = lax.associative_scan(combine, (ar, ai, bur, bui), axis=1)
        cr, ci = _cmul(pr, pi_, sr[:, None], si[:, None])
        xr, xi = xr + cr, xi + ci
        y = (jnp.einsum('btgn,gpn->btgp', xr, cr_w)
             - jnp.einsum('btgn,gpn->btgp', xi, ci_w) + d * u_c)
        return (xr[:, -1], xi[:, -1]), y

    init = (jnp.zeros((bsz, SSM_GROUPS, SSM_STATE), f32),
            jnp.zeros((bsz, SSM_GROUPS, SSM_STATE), f32))
    _, ys = lax.scan(chunk_step, init, u)
    return jnp.moveaxis(ys, 0, 1).reshape(bsz, seq, SSM_WIDTH).astype(out_dtype)


def _rope(t, cos, sin):
    half = t.shape[-1] // 2
    t1, t2 = t[..., :half], t[..., half:]
    return jnp.concatenate([t1 * cos - t2 * sin, t2 * cos + t1 * sin], axis=-1).astype(t.dtype)


def _mla_branch(q_lat, kv_lat, k_rope, positions, g_q_norm, w_q_up, g_kv_norm, w_kv_up):
    bsz, seq, _ = q_lat.shape
    q = (_rmsnorm(q_lat, g_q_norm) @ w_q_up).reshape(bsz, seq, MLA_HEADS, QK_NOPE + QK_ROPE)
    q_nope, q_rope = q[..., :QK_NOPE], q[..., QK_NOPE:]
    kv = (_rmsnorm(kv_lat, g_kv_norm) @ w_kv_up).reshape(bsz, seq, MLA_HEADS, QK_NOPE + V_HEAD)
    k_nope, v = kv[..., :QK_NOPE], kv[..., QK_NOPE:]
    inv_freq = ROPE_BASE ** (-jnp.arange(0, QK_ROPE, 2, dtype=jnp.float32) / QK_ROPE)
    ang = positions.astype(jnp.float32)[..., None] * inv_freq
    cos, sin = jnp.cos(ang), jnp.sin(ang)
    q_rope = _rope(q_rope, cos[:, :, None], sin[:, :, None])
    k_rope = _rope(k_rope, cos, sin)
    n_blk = seq // Q_BLOCK
    key_chunk = jnp.arange(seq) // CHUNK
    scale = (QK_NOPE + QK_ROPE) ** -0.5

    def to_blocks(t):
        return jnp.moveaxis(t.reshape(bsz, n_blk, Q_BLOCK, *t.shape[2:]), 1, 0)

    def attend(args):
        qn, qr, blk = args
        s = (jnp.einsum('bqhd,bkhd->bhqk', qn, k_nope)
             + jnp.einsum('bqhr,bkr->bhqk', qr, k_rope)).astype(jnp.float32) * scale
        q_chunk = (blk * Q_BLOCK + jnp.arange(Q_BLOCK)) // CHUNK
        mask = key_chunk[None, :] <= q_chunk[:, None]
        p = jax.nn.softmax(jnp.where(mask, s, -jnp.inf), axis=-1).astype(v.dtype)
        return jnp.einsum('bhqk,bkhd->bqhd', p, v)

    o = lax.map(attend, (to_blocks(q_nope), to_blocks(q_rope), jnp.arange(n_blk)))
    return jnp.moveaxis(o, 0, 1).reshape(bsz, seq, MLA_WIDTH)


def setup_inputs(seed: int = 0) -> dict:
    key = jax.random.key(seed)
    ks = jax.random.split(key, 25)
    f32 = jnp.float32

    def nrm(k, shape, scale):
        return jax.random.normal(k, shape, f32) * scale

    positions = (jax.random.randint(ks[2], (BATCH, 1), 0, 2048, dtype=jnp.int32)
                 + jnp.arange(SEQ, dtype=jnp.int32)[None, :])
    return {
        "x": nrm(ks[0], (BATCH, SEQ, D_MODEL), 1.0),
        "c": nrm(ks[1], (BATCH, D_MODEL), 1.0),
        "positions": positions,
        "w_ada": nrm(ks[3], (DEPTH, D_MODEL, 3 * D_MODEL), 0.5 * D_MODEL ** -0.5),
        "b_ada": nrm(ks[4], (DEPTH, 3 * D_MODEL), 0.01),
        "g_pre": 1.0 + nrm(ks[5], (DEPTH, D_MODEL), 0.01),
        "w_in": nrm(ks[6], (DEPTH, D_MODEL, IN_WIDTH), D_MODEL ** -0.5),
        "ssm_log_dt": jax.random.uniform(ks[7], (DEPTH, SSM_GROUPS), f32,
                                         math.log(DT_MIN), math.log(DT_MAX)),
        "ssm_a_re": -0.5 + nrm(ks[8], (DEPTH, SSM_GROUPS, SSM_STATE), 0.01),
        "ssm_a_im": (jnp.pi * jnp.arange(SSM_STATE, dtype=f32)
                     + nrm(ks[9], (DEPTH, SSM_GROUPS, SSM_STATE), 0.01)),
        "ssm_b_re": nrm(ks[10], (DEPTH, SSM_GROUPS, SSM_STATE, SSM_GROUP), (2 * SSM_GROUP) ** -0.5),
        "ssm_b_im": nrm(ks[11], (DEPTH, SSM_GROUPS, SSM_STATE, SSM_GROUP), (2 * SSM_GROUP) ** -0.5),
        "ssm_c_re": nrm(ks[12], (DEPTH, SSM_GROUPS, SSM_GROUP, SSM_STATE), (2 * SSM_STATE) ** -0.5),
        "ssm_c_im": nrm(ks[13], (DEPTH, SSM_GROUPS, SSM_GROUP, SSM_STATE), (2 * SSM_STATE) ** -0.5),
        "ssm_d": nrm(ks[14], (DEPTH, SSM_WIDTH), 0.5),
        "w_glu": nrm(ks[15], (DEPTH, SSM_WIDTH, 2 * SSM_WIDTH), SSM_WIDTH ** -0.5),
        "b_glu": nrm(ks[16], (DEPTH, 2 * SSM_WIDTH), 0.01),
        "g_q_norm": 1.0 + nrm(ks[17], (DEPTH, Q_LORA), 0.01),
        "w_q_up": nrm(ks[18], (DEPTH, Q_LORA, MLA_HEADS * (QK_NOPE + QK_ROPE)), Q_LORA ** -0.5),
        "g_kv_norm": 1.0 + nrm(ks[19], (DEPTH, KV_LORA), 0.01),
        "w_kv_up": nrm(ks[20], (DEPTH, KV_LORA, MLA_HEADS * (QK_NOPE + V_HEAD)), KV_LORA ** -0.5),
        "w_br_ssm": nrm(ks[21], (DEPTH, SSM_WIDTH, D_MODEL), SSM_WIDTH ** -0.5),
        "w_br_mla": nrm(ks[22], (DEPTH, MLA_WIDTH, D_MODEL), MLA_WIDTH ** -0.5),
        "w_out": nrm(ks[23], (DEPTH, D_MODEL, D_MODEL), D_MODEL ** -0.5),
        "g_post": 1.0 + nrm(ks[24], (DEPTH, D_MODEL), 0.01),
    }


def reference(x, c, positions, w_ada, b_ada, g_pre, w_in, ssm_log_dt, ssm_a_re, ssm_a_im,
              ssm_b_re, ssm_b_im, ssm_c_re, ssm_c_im, ssm_d, w_glu, b_glu, g_q_norm, w_q_up,
              g_kv_norm, w_kv_up, w_br_ssm, w_br_mla, w_out, g_post):
    split_at = np.cumsum(IN_SPLITS)[:-1].tolist()
    for l in range(DEPTH):
        mod = c @ w_ada[l] + b_ada[l]
        shift, scale, gate = jnp.split(mod, 3, axis=-1)
        h = _rmsnorm(x, g_pre[l]) * (1.0 + scale[:, None]) + shift[:, None]
        proj = h @ w_in[l]
        u_s, z_s, q_lat, kv_lat, k_rope, z_m, gl_s, gl_m = jnp.split(proj, split_at, axis=-1)
        y_s = _ssm_branch(u_s, ssm_log_dt[l], ssm_a_re[l], ssm_a_im[l], ssm_b_re[l],
                          ssm_b_im[l], ssm_c_re[l], ssm_c_im[l], ssm_d[l])
        glu_a, glu_b = jnp.split(jax.nn.gelu(y_s) @ w_glu[l] + b_glu[l], 2, axis=-1)
        y_s = ((glu_a * jax.nn.sigmoid(glu_b)) * jax.nn.silu(z_s)) @ w_br_ssm[l]
        y_m = _mla_branch(q_lat, kv_lat, k_rope, positions, g_q_norm[l], w_q_up[l],
                          g_kv_norm[l], w_kv_up[l])
        y_m = (y_m * jax.nn.silu(z_m)) @ w_br_mla[l]
        merged = jax.nn.sigmoid(gl_s) * y_s + jax.nn.sigmoid(gl_m) * y_m
        out = merged @ w_out[l]
        x = x + gate[:, None] * _rmsnorm(out, g_post[l])
    return x
```

```python
import math
from contextlib import ExitStack
import numpy as np
import concourse.bass as bass
import concourse.mybir as mybir
from concourse.bass_utils import run_bass_kernel_spmd

F32 = mybir.dt.float32
BF16 = mybir.dt.bfloat16
I32 = mybir.dt.int32
ALU = mybir.AluOpType
AF = mybir.ActivationFunctionType

D = 1024
S = 4096
NTB = 8
TB = 512
INW = 5664
EPS = 1e-6
OFF_U, OFF_ZS, OFF_Q, OFF_KV, OFF_KR, OFF_ZM, OFF_GLS, OFF_GLM = 0, 1024, 2048, 2304, 2560, 2592, 3616, 4640
ATT_SCALE = 96 ** -0.5
TWO_PI = 2.0 * math.pi
C1 = 6.28125
C2 = TWO_PI - C1
SIN_CLAMP = 3.141592
SAME_ENGINE_SYNC = True
SEM_LIMIT = 30000
NDS = 40

DEBUG_OUT = False
import os
L1MODE = int(os.environ.get('L1MODE', '0'))
SPLIT_PV = False
FILLER_N = int(os.environ.get('FILLER_N', '0'))


class Res:
    __slots__ = ("name", "w", "r")

    def __init__(self, name=""):
        self.name = name
        self.w = None
        self.r = []


class Sched:
    def __init__(self, nc):
        self.nc = nc
        self.E = {"pe": nc.tensor, "act": nc.scalar, "dve": nc.vector, "pool": nc.gpsimd, "sp": nc.sync}
        self.gen = {e: 0 for e in self.E}
        self.sem = {e: nc.alloc_semaphore("s_%s_0" % e) for e in self.E}
        self.cnt = {e: 0 for e in self.E}
        self.waited = {e: {} for e in self.E}
        self.dsems = [nc.alloc_semaphore("d%d" % i) for i in range(NDS)]
        self.dcnt = [0] * NDS
        self.dq = {"sp": list(range(0, 16)), "pool": list(range(NDS - 10, NDS))}
        self.dnext = {"sp": 0, "pool": 0}
        self.old = []
        self.nops = 0

    def _wait(self, e, tok):
        key, sem, val = tok
        if self.waited[e].get(key, 0) >= val:
            return
        self.E[e].wait_ge(sem, val)
        self.waited[e][key] = val

    def _collect(self, e, reads, writes):
        toks = []
        for r in reads:
            if r.w is not None:
                toks.append(r.w)
        for w in writes:
            if w.w is not None:
                toks.append(w.w)
            toks.extend(w.r)
        for tok in toks:
            key = tok[0]
            if isinstance(key, tuple) and key[0] == e:
                if e in ("pe", "sp") or not SAME_ENGINE_SYNC:
                    continue
            self._wait(e, tok)

    def _mark(self, tok, reads, writes):
        for r in reads:
            r.r.append(tok)
        for w in writes:
            w.w = tok
            w.r = []

    def do(self, e, fn, reads=(), writes=()):
        self._collect(e, reads, writes)
        if self.cnt[e] >= SEM_LIMIT:
            self.old.append(((e, self.gen[e]), self.sem[e], self.cnt[e]))
            self.gen[e] += 1
            self.sem[e] = self.nc.alloc_semaphore("s_%s_%d" % (e, self.gen[e]))
            self.cnt[e] = 0
        inst = fn(self.E[e])
        self.cnt[e] += 1
        inst.then_inc(self.sem[e], 1)
        tok = ((e, self.gen[e]), self.sem[e], self.cnt[e])
        self._mark(tok, reads, writes)
        self.nops += 1
        return tok

    def dma(self, q, fn, reads=(), writes=(), n=1):
        self._collect(q, reads, writes)
        lst = self.dq[q]
        i = lst[self.dnext[q] % len(lst)]
        self.dnext[q] += 1
        if self.dcnt[i] > 0:
            self._wait(q, (("d", i), self.dsems[i], self.dcnt[i]))
        insts = fn(self.E[q])
        if not isinstance(insts, (list, tuple)):
            insts = [insts]
        for inst in insts:
            inst.then_inc(self.dsems[i], 16)
            self.dcnt[i] += 16
        tok = (("d", i), self.dsems[i], self.dcnt[i])
        self._mark(tok, reads, writes)
        return tok

    def barrier(self):
        toks = []
        for e in self.E:
            if self.cnt[e] > 0:
                toks.append(((e, self.gen[e]), self.sem[e], self.cnt[e]))
        toks.extend(self.old)
        for i in range(NDS):
            if self.dcnt[i] > 0:
                toks.append((("d", i), self.dsems[i], self.dcnt[i]))
        for e in self.E:
            for tok in toks:
                if tok[0][0] == e:
                    continue
                self._wait(e, tok)


class _Stop(Exception):
    pass


def build(debug=False, upto=99):
    try:
        return _build(debug, upto)
    except _Stop as st:
        return st.args[0]


def _build(debug=False, upto=99):
    nc = bass.Bass("TRN2", target_bir_lowering=False)
    s = Sched(nc)

    def chk(v):
        if upto == v:
            s.barrier()
            raise _Stop(nc)

    def din(name, shape, dt=F32):
        return nc.dram_tensor(name, list(shape), dt, kind="ExternalInput").ap()

    xT = din("xT", [D, S])
    cT = din("cT", [128, 8])
    pos32 = din("pos32", [32, S], I32)
    w_ada = din("w_ada", [D, 3 * D])
    b_adaT = din("b_adaT", [128, 24])
    g_preT = din("g_preT", [128, 8])
    w_in = din("w_in", [D, INW])
    lrT = din("lrT", [128, 64])
    liT = din("liT", [128, 64])
    ldt = din("ldt", [128, 64])
    Bpad = din("Bpad", [64, 128, 128])
    Cpad = din("Cpad", [64, 128, 128])
    dT = din("dT", [128, 8])
    w_glu = din("w_glu", [D, 2 * D])
    b_gluT = din("b_gluT", [128, 16])
    g_qT = din("g_qT", [128, 2])
    w_q = din("w_q", [256, 1536])
    g_kvT = din("g_kvT", [128, 2])
    w_kv = din("w_kv", [256, 2048])
    w_brs = din("w_brs", [D, D])
    w_brm = din("w_brm", [D, D])
    w_o = din("w_o", [D, D])
    g_postT = din("g_postT", [128, 8])
    cI = din("cI", [128, 128])
    cP = din("cP", [128, 128])
    cSgn = din("cSgn", [128, 1])
    cR = din("cR", [128, 96])
    cInvf = din("cInvf", [128, 1])

    outT = nc.dram_tensor("outT", [D, S], F32, kind="ExternalOutput").ap()
    kind_scr = "ExternalOutput" if debug else "Internal"
    projT = nc.dram_tensor("projT", [INW, S], BF16, kind=kind_scr).ap()
    gyT = nc.dram_tensor("gyT", [D, S], BF16, kind=kind_scr).ap()
    msT = nc.dram_tensor("msT", [D, S], BF16, kind=kind_scr).ap()
    ymT = nc.dram_tensor("ymT", [D, S], BF16, kind=kind_scr).ap()

    es_glob = ExitStack()

    def sb(es, name, shape, dt):
        return es.enter_context(nc.sbuf_tensor(name, list(shape), dt))

    PS = [es_glob.enter_context(nc.psum_tensor("ps%d" % i, [128, 512], F32)) for i in range(8)]
    PSR = [Res("ps%d" % i) for i in range(8)]

    ident = sb(es_glob, "ident", [128, 128], F32)
    swapP = sb(es_glob, "swapP", [128, 128], F32)
    sgn = sb(es_glob, "sgn", [128, 1], F32)
    ones_bf = sb(es_glob, "ones_bf", [128, 128], BF16)
    mod = sb(es_glob, "mod", [128, 24], F32)
    gm = sb(es_glob, "gm", [128, 8], F32)
    gg = sb(es_glob, "gg", [128, 8], F32)
    shift_bf = sb(es_glob, "shift_bf", [128, 8], BF16)
    swb = sb(es_glob, "swb", [128, 48], F32)
    r_const = Res("const")
    r_mod = Res("mod")

    def ld(q, out_ap, in_ap, writes, reads=()):
        return s.dma(q, lambda e: e.dma_start(out=out_ap, in_=in_ap), reads=reads, writes=writes)

    ld("sp", ident[:], cI, [r_const])
    ld("sp", swapP[:], cP, [r_const])
    ld("sp", sgn[:], cSgn, [r_const])
    s.do("dve", lambda e: e.memset(ones_bf[:], 1.0), writes=[r_const])

    eps_t = sb(es_glob, "eps_t", [128, 1], F32)
    EPS_AP = eps_t
    s.do("dve", lambda e: e.memset(eps_t[:], EPS), writes=[r_const])
    es_wi = ExitStack()
    wi = sb(es_wi, "wi", [128, 8, INW], BF16)
    r_wi = [Res("wi%d" % k) for k in range(8)]
    for kc in range(8):
        ld("pool", wi[:, kc, :], w_in[kc * 128:(kc + 1) * 128, :], [r_wi[kc]])
    with ExitStack() as es:
        wa = sb(es, "wa", [128, 8, 3 * D], F32)
        c_sb = sb(es, "c_sb", [128, 8], F32)
        bada = sb(es, "bada", [128, 24], F32)
        gpre = sb(es, "gpre", [128, 8], F32)
        gpost = sb(es, "gpost", [128, 8], F32)
        tmp8 = sb(es, "tmp8", [128, 8], F32)
        r_wa = [Res("wa%d" % k) for k in range(8)]
        r_small = Res("small0")
        for kc in range(8):
            ld("sp", wa[:, kc, :], w_ada[kc * 128:(kc + 1) * 128, :], [r_wa[kc]])
        ld("sp", c_sb[:], cT, [r_small])
        ld("sp", bada[:], b_adaT, [r_small])
        ld("sp", gpre[:], g_preT, [r_small])
        ld("sp", gpost[:], g_postT, [r_small])

        def mm0(pe):
            inst = None
            for j in range(24):
                for kc in range(8):
                    inst = pe.matmul(PS[0][:, j:j + 1], wa[:, kc, j * 128:(j + 1) * 128], c_sb[:, kc:kc + 1],
                                     start=(kc == 0), stop=(kc == 7))
            return inst
        s.do("pe", mm0, reads=r_wa + [r_small], writes=[PSR[0]])
        s.do("dve", lambda e: e.tensor_tensor(mod[:], PS[0][:, 0:24], bada[:], ALU.add), reads=[PSR[0], r_small], writes=[r_mod])
        s.do("dve", lambda e: e.tensor_scalar(tmp8[:], mod[:, 8:16], 1.0, 0.0, ALU.add, ALU.add), reads=[r_mod], writes=[r_small])
        s.do("dve", lambda e: e.tensor_tensor(gm[:], tmp8[:], gpre[:], ALU.mult), reads=[r_small], writes=[r_mod])
        s.do("dve", lambda e: e.tensor_tensor(gg[:], mod[:, 16:24], gpost[:], ALU.mult), reads=[r_small, r_mod], writes=[r_mod])
        s.do("dve", lambda e: e.tensor_copy(shift_bf[:], mod[:, 0:8]), reads=[r_mod], writes=[r_mod])
        s.barrier()

    bank = [0]

    nb_mod = [8]

    def nextbank():
        b = bank[0] % nb_mod[0]
        bank[0] = (b + 1) % nb_mod[0]
        return b

    def rsqrt_mean(ps_ap, out_ap, n, r_ps, r_out, tmp_ap, r_tmp):
        s.do("act", lambda e: e.activation(out=tmp_ap, in_=ps_ap, func=AF.Sqrt, bias=EPS_AP[:, 0:1], scale=1.0 / n), reads=[r_ps], writes=[r_tmp])
        s.do("dve", lambda e: e.reciprocal(out_ap, tmp_ap), reads=[r_tmp], writes=[r_out])


    if upto == 0:
        es_glob.close()
        raise _Stop(nc)
    with ExitStack() as es:
        xt = [sb(es, "xt%d" % i, [128, 8, TB], F32) for i in range(2)]
        r_xt = [Res("xt%d" % i) for i in range(2)]
        sq = sb(es, "sq", [128, 8, TB], BF16)
        r_sq = Res("sq")
        hT = [sb(es, "hT%d" % i, [128, 8, TB], BF16) for i in range(2)]
        r_hT = [Res("hT%d" % i) for i in range(2)]
        rstd = sb(es, "rstd", [128, TB], F32)
        r_rstd = Res("rstd")
        rtmp = sb(es, "rtmp", [128, TB], F32)
        r_rtmp = Res("rtmp")
        ost = [sb(es, "ost%d" % i, [128, TB], BF16) for i in range(4)]
        r_ost = [Res("ost%d" % i) for i in range(4)]
        noc = (INW + 127) // 128

        def mmsw(pe):
            inst = None
            for oc in range(noc):
                m = min(128, INW - oc * 128)
                for kc in range(8):
                    inst = pe.matmul(PS[1][0:m, oc:oc + 1], wi[:, kc, oc * 128:oc * 128 + m], shift_bf[:, kc:kc + 1],
                                     start=(kc == 0), stop=(kc == 7))
            return inst
        s.do("dve", lambda e: e.memset(swb[:], 0.0), writes=[r_mod])
        s.do("pe", mmsw, reads=r_wi + [r_mod], writes=[PSR[1]])
        s.do("dve", lambda e: e.tensor_copy(swb[:, 0:noc - 1], PS[1][:, 0:noc - 1]), reads=[PSR[1]], writes=[r_mod])
        s.do("dve", lambda e: e.tensor_copy(swb[0:32, noc - 1:noc], PS[1][0:32, noc - 1:noc]), reads=[PSR[1]], writes=[r_mod])

        xTv = xT.rearrange("(kc p) t -> p kc t", p=128)
        oi = 0
        def load1(tbb):
            ld("sp", xt[tbb % 2][:], xTv[:, :, tbb * TB:(tbb + 1) * TB], [r_xt[tbb % 2]])
        load1(0)
        for tb in range(NTB):
            t0 = tb * TB
            xb = xt[tb % 2]
            rx = r_xt[tb % 2]
            if tb + 1 < NTB:
                load1(tb + 1)
            def stats1(tbb):
                xb_ = xt[tbb % 2]
                rx_ = r_xt[tbb % 2]
                s.do("act", lambda e: e.activation(out=sq[:], in_=xb_[:], func=AF.Square), reads=[rx_], writes=[r_sq])
                b_ = nextbank()

                def mmss(pe):
                    inst = None
                    for kc in range(8):
                        inst = pe.matmul(PS[b_][:], ones_bf[:], sq[:, kc, :], start=(kc == 0), stop=(kc == 7))
                    return inst
                s.do("pe", mmss, reads=[r_sq, r_const], writes=[PSR[b_]])
                rsqrt_mean(PS[b_][:], rstd[:], D, PSR[b_], r_rstd, rtmp[:], r_rtmp)
                h_ = hT[tbb % 2]
                rh_ = r_hT[tbb % 2]
                for kc in range(8):
                    s.do("dve", lambda e, kc=kc: e.scalar_tensor_tensor(h_[:, kc, :], xb_[:, kc, :], gm[:, kc:kc + 1], rstd[:], ALU.mult, ALU.mult),
                         reads=[rx_, r_rstd, r_mod], writes=[rh_])
            if tb == 0:
                stats1(0)
            h = hT[tb % 2]
            rh = r_hT[tb % 2]
            for oc in range(noc):
                if oc == 24 and tb + 1 < NTB:
                    stats1(tb + 1)
                m = min(128, INW - oc * 128)
                b = nextbank()

                def mmp(pe, b=b, oc=oc, m=m):
                    inst = None
                    for kc in range(8):
                        inst = pe.matmul(PS[b][0:m, :], wi[:, kc, oc * 128:oc * 128 + m], h[:, kc, :], start=(kc == 0), stop=(kc == 7))
                    return inst
                s.do("pe", mmp, reads=r_wi + [rh], writes=[PSR[b]])
                o = ost[oi % 4]
                ro = r_ost[oi % 4]
                oi += 1
                segs = []
                p0 = 0
                while p0 < m:
                    r = oc * 128 + p0
                    if OFF_ZS <= r < OFF_Q or OFF_ZM <= r < OFF_GLS:
                        fn = AF.Silu
                    elif r >= OFF_GLS:
                        fn = AF.Sigmoid
                    else:
                        fn = None
                    p1 = p0 + 32
                    segs.append([p0, p1, fn])
                    p0 = p1
                merged = []
                for sg in segs:
                    if merged and merged[-1][2] == sg[2]:
                        merged[-1][1] = sg[1]
                    else:
                        merged.append(sg)
                legal = []
                for (a0, a1, fn) in merged:
                    while a0 < a1:
                        ln = min({0: 128, 32: 32, 64: 64, 96: 32}[a0], a1 - a0)
                        legal.append((a0, a0 + ln, fn))
                        a0 += ln
                for (q0, q1, fn) in legal:
                    if fn is None and oc % 2 == 1:
                        s.do("dve", lambda e, b=b, q0=q0, q1=q1, oc=oc, o=o: e.tensor_scalar(o[q0:q1, :], PS[b][q0:q1, :], swb[q0:q1, oc:oc + 1], 0.0, ALU.add, ALU.add),
                             reads=[PSR[b], r_mod], writes=[ro])
                    else:
                        f = AF.Identity if fn is None else fn
                        s.do("act", lambda e, b=b, q0=q0, q1=q1, oc=oc, o=o, f=f: e.activation(out=o[q0:q1, :], in_=PS[b][q0:q1, :], func=f, bias=swb[q0:q1, oc:oc + 1], scale=1.0),
                             reads=[PSR[b], r_mod], writes=[ro])
                s.dma("sp", lambda e, o=o, m=m, oc=oc: e.dma_start(out=projT[oc * 128:oc * 128 + m, t0:t0 + TB], in_=o[0:m, :]), reads=[ro])
        s.barrier()
    es_wi.close()

    if upto == 1:
        es_glob.close()
        raise _Stop(nc)
    nb_mod[0] = 6
    with ExitStack() as es:
        def ct(name):
            return sb(es, name, [128, 64], F32)
        r_co = Res("coef")
        lr_t, li_t, dt_t = ct("lr_t"), ct("li_t"), ct("dt_t")
        ld("sp", lr_t[:], lrT, [r_co])
        ld("sp", li_t[:], liT, [r_co])
        ld("sp", dt_t[:], ldt, [r_co])
        dsk = sb(es, "dsk", [128, 8], F32)
        ld("sp", dsk[:], dT, [r_co])
        tA, tB, tC, tD = ct("tA"), ct("tB"), ct("tC"), ct("tD")
        tI = sb(es, "tI", [128, 64], I32)

        def V(fn):
            s.do("dve", fn, reads=[r_co], writes=[r_co])

        def A(fn):
            s.do("act", fn, reads=[r_co], writes=[r_co])

        def sin_of(out_t, ang_t):
            V(lambda e: e.tensor_scalar(tA[:], ang_t[:], 1.0 / TWO_PI, 0.0, ALU.mult, ALU.add))
            V(lambda e: e.tensor_copy(tI[:], tA[:]))
            V(lambda e: e.tensor_copy(tA[:], tI[:]))
            V(lambda e: e.scalar_tensor_tensor(out_t[:], tA[:], -C1, ang_t[:], ALU.mult, ALU.add))
            V(lambda e: e.scalar_tensor_tensor(out_t[:], tA[:], -C2, out_t[:], ALU.mult, ALU.add))
            V(lambda e: e.tensor_scalar(out_t[:], out_t[:], SIN_CLAMP, -SIN_CLAMP, ALU.min, ALU.max))
            A(lambda e: e.activation(out=out_t[:], in_=out_t[:], func=AF.Sin))

        A(lambda e: e.activation(out=dt_t[:], in_=dt_t[:], func=AF.Exp))
        mag, ang = ct("mag"), ct("ang")
        V(lambda e: e.tensor_tensor(mag[:], lr_t[:], dt_t[:], ALU.mult))
        A(lambda e: e.activation(out=mag[:], in_=mag[:], func=AF.Exp))
        V(lambda e: e.tensor_tensor(ang[:], li_t[:], dt_t[:], ALU.mult))
        ER, EI = {}, {}
        ER[0], EI[0], ER[1], EI[1] = ct("er0"), ct("ei0"), ct("er1"), ct("ei1")
        V(lambda e: e.memset(ER[0][:], 1.0))
        V(lambda e: e.memset(EI[0][:], 0.0))
        sin_of(EI[1], ang)
        V(lambda e: e.tensor_scalar(tB[:], ang[:], math.pi / 2, 0.0, ALU.add, ALU.add))
        sin_of(ER[1], tB)
        V(lambda e: e.tensor_tensor(ER[1][:], ER[1][:], mag[:], ALU.mult))
        V(lambda e: e.tensor_tensor(EI[1][:], EI[1][:], mag[:], ALU.mult))

        def cmul(orr, oii, ar, ai, br, bi):
            V(lambda e: e.tensor_tensor(tC[:], ai[:], bi[:], ALU.mult))
            V(lambda e: e.tensor_tensor(tD[:], ai[:], br[:], ALU.mult))
            V(lambda e: e.tensor_tensor(tA[:], ar[:], br[:], ALU.mult))
            V(lambda e: e.tensor_tensor(tB[:], ar[:], bi[:], ALU.mult))
            V(lambda e: e.tensor_tensor(orr[:], tA[:], tC[:], ALU.subtract))
            V(lambda e: e.tensor_tensor(oii[:], tB[:], tD[:], ALU.add))

        for k in range(2, 9):
            ER[k], EI[k] = ct("er%d" % k), ct("ei%d" % k)
            cmul(ER[k], EI[k], ER[k - 1], EI[k - 1], ER[1], EI[1])
        k = 8
        while k < 2048:
            ER[2 * k], EI[2 * k] = ct("er%d" % (2 * k)), ct("ei%d" % (2 * k))
            cmul(ER[2 * k], EI[2 * k], ER[k], EI[k], ER[k], EI[k])
            k *= 2
        den, nr, fr, fi = ct("den"), ct("nr"), ct("fr"), ct("fi")
        V(lambda e: e.tensor_tensor(den[:], lr_t[:], lr_t[:], ALU.mult))
        V(lambda e: e.tensor_tensor(tA[:], li_t[:], li_t[:], ALU.mult))
        V(lambda e: e.tensor_tensor(den[:], den[:], tA[:], ALU.add))
        V(lambda e: e.reciprocal(den[:], den[:]))
        V(lambda e: e.tensor_scalar(nr[:], ER[1][:], -1.0, 0.0, ALU.add, ALU.add))
        V(lambda e: e.tensor_tensor(tA[:], nr[:], lr_t[:], ALU.mult))
        V(lambda e: e.tensor_tensor(tB[:], EI[1][:], li_t[:], ALU.mult))
        V(lambda e: e.tensor_tensor(fr[:], tA[:], tB[:], ALU.add))
        V(lambda e: e.tensor_tensor(fr[:], fr[:], den[:], ALU.mult))
        V(lambda e: e.tensor_tensor(tA[:], EI[1][:], lr_t[:], ALU.mult))
        V(lambda e: e.tensor_tensor(tB[:], nr[:], li_t[:], ALU.mult))
        V(lambda e: e.tensor_tensor(fi[:], tA[:], tB[:], ALU.subtract))
        V(lambda e: e.tensor_tensor(fi[:], fi[:], den[:], ALU.mult))
        GR, GIs = {}, {}
        for tau in range(8):
            GR[tau], GIs[tau] = ct("gr%d" % tau), ct("gi%d" % tau)
            cmul(GR[tau], GIs[tau], ER[tau], EI[tau], fr, fi)
            V(lambda e, tau=tau: e.tensor_scalar(GIs[tau][:], GIs[tau][:], sgn[:, 0:1], 0.0, ALU.mult, ALU.add))
        ERs, EIn, EIs = {}, {}, {}
        for k in range(0, 9):
            ERs[k], EIn[k] = ct("ers%d" % k), ct("ein%d" % k)
            V(lambda e, k=k: e.tensor_scalar(ERs[k][:], ER[k][:], sgn[:, 0:1], 0.0, ALU.mult, ALU.add))
            V(lambda e, k=k: e.tensor_scalar(EIn[k][:], EI[k][:], -1.0, 0.0, ALU.mult, ALU.add))
        LEV = [8 * 2 ** k for k in range(9)]
        for p in LEV:
            EIs[p] = ct("eis%d" % p)
            V(lambda e, p=p: e.tensor_scalar(EIs[p][:], EI[p][:], sgn[:, 0:1], 0.0, ALU.mult, ALU.add))

        chk(2.1)
        GRall = sb(es, "GRall", [128, 64, 8], F32)
        GIall = sb(es, "GIall", [128, 64, 8], F32)
        O1all = sb(es, "O1all", [128, 64, 9], F32)
        O2all = sb(es, "O2all", [128, 64, 9], F32)
        M1all = sb(es, "M1all", [128, 64, 9], F32)
        M2all = sb(es, "M2all", [128, 64, 9], F32)
        for tau in range(8):
            V(lambda e, tau=tau: e.tensor_copy(GRall[:, :, tau], GR[tau][:]))
            V(lambda e, tau=tau: e.tensor_copy(GIall[:, :, tau], GIs[tau][:]))
        for k in range(9):
            V(lambda e, k=k: e.tensor_copy(O1all[:, :, k], ERs[k][:]))
            V(lambda e, k=k: e.tensor_copy(O2all[:, :, k], EIn[k][:]))
            V(lambda e, k=k: e.tensor_copy(M1all[:, :, k], ER[LEV[k]][:]))
            V(lambda e, k=k: e.tensor_copy(M2all[:, :, k], EIs[LEV[k]][:]))
        bp = [sb(es, "bp%d" % i, [128, 128], BF16) for i in range(4)]
        cp = [sb(es, "cp%d" % i, [128, 128], BF16) for i in range(4)]
        r_bp = [Res() for _ in range(4)]
        r_cp = [Res() for _ in range(4)]
        Dall = [sb(es, "Dall%d" % i, [128, 8, 128], BF16) for i in range(4)]
        D2all = [sb(es, "D2all%d" % i, [128, 9, 128], BF16) for i in range(4)]
        Dtmp = sb(es, "Dtmp", [128, 9, 128], F32)
        Dtmpb = sb(es, "Dtmpb", [128, 9, 128], BF16)
        r_Dtmpb = Res()
        Dtmpb2 = sb(es, "Dtmpb2", [128, 9, 128], BF16)
        r_Dtmpb2 = Res()
        r_Dall = [Res() for _ in range(4)]
        r_D2all = [Res() for _ in range(4)]
        r_Dtmp = Res()
        LAg = [sb(es, "LAg%d" % i, [128, 8, 128], BF16) for i in range(4)]
        r_LAg = [Res() for _ in range(4)]
        MTg = [sb(es, "MTg%d" % i, [128, 9, 128], F32) for i in range(6)]
        r_MTg = [Res() for _ in range(6)]
        AB = sb(es, "AB", [128, 8, 128], BF16)
        r_AB = Res()
        OCm1 = sb(es, "OCm1", [128, 128], BF16)
        r_OCm1 = Res()
        OCt = [sb(es, "OCt%d" % i, [128, 8, 8, 128], BF16) for i in range(2)]
        r_OCt = [[Res() for _ in range(8)] for _ in range(2)]
        Tm = [sb(es, "Tm%d" % i, [128, 8, 128], BF16) for i in range(2)]
        r_Tm = [Res() for _ in range(2)]
        uT = [sb(es, "uT%d" % i, [128, S], BF16) for i in range(2)]
        r_uT = [Res() for _ in range(2)]
        yT = [sb(es, "yT%d" % i, [128, S], BF16) for i in range(1)] * 2
        r_yT = [Res()] * 2
        S32g = [sb(es, "S32g%d" % i, [128, 512], F32) for i in range(2)]
        r_S32g = [Res() for _ in range(2)]
        Sb = [sb(es, "Sb%d" % i, [128, 8, 512], BF16) for i in range(2)]
        r_Sb = [[Res() for _ in range(8)] for _ in range(2)]
        identB8 = ident[:].unsqueeze(1).broadcast_to([128, 8, 128])
        swapB8 = swapP[:].unsqueeze(1).broadcast_to([128, 8, 128])
        identB9 = ident[:].unsqueeze(1).broadcast_to([128, 9, 128])
        swapB9 = swapP[:].unsqueeze(1).broadcast_to([128, 9, 128])

        def bc(t, g, n):
            return t[:, g, :].unsqueeze(2).broadcast_to([128, n, 128])

        def A_build(g):
            p = g % 4
            p3 = g % 6
            rc = [r_co, r_const]
            ops = []

            def o0():
                ld("pool", bp[p][:], Bpad[g], [r_bp[p]])
                ld("pool", cp[p][:], Cpad[g], [r_cp[p]])
                s.do("dve", lambda e: e.tensor_tensor(Dall[p][:], identB8, bc(GRall, g, 8), ALU.mult), reads=rc, writes=[r_Dall[p]])
            ops.append(o0)
            ops.append(lambda: s.do("pool", lambda e: e.tensor_tensor(Dtmpb[:, 0:8, :], swapB8, bc(GIall, g, 8), ALU.mult), reads=rc, writes=[r_Dtmpb]))
            ops.append(lambda: s.do("dve", lambda e: e.tensor_tensor(Dall[p][:], Dall[p][:], Dtmpb[:, 0:8, :], ALU.add), reads=[r_Dtmpb, r_Dall[p]], writes=[r_Dall[p]]))
            ops.append(lambda: s.do("dve", lambda e: e.tensor_tensor(D2all[p][:], identB9, bc(O1all, g, 9), ALU.mult), reads=rc, writes=[r_D2all[p]]))
            ops.append(lambda: s.do("pool", lambda e: e.tensor_tensor(Dtmpb2[:], swapB9, bc(O2all, g, 9), ALU.mult), reads=rc, writes=[r_Dtmpb2]))
            ops.append(lambda: s.do("dve", lambda e: e.tensor_tensor(D2all[p][:], D2all[p][:], Dtmpb2[:], ALU.add), reads=[r_Dtmpb2, r_D2all[p]], writes=[r_D2all[p]]))
            ops.append(lambda: s.do("dve", lambda e: e.tensor_tensor(MTg[p3][:], identB9, bc(M1all, g, 9), ALU.mult), reads=rc, writes=[r_MTg[p3]]))
            ops.append(lambda: s.do("pool", lambda e: e.tensor_tensor(Dtmp[:], swapB9, bc(M2all, g, 9), ALU.mult), reads=rc, writes=[r_Dtmp]))
            ops.append(lambda: s.do("dve", lambda e: e.tensor_tensor(MTg[p3][:], MTg[p3][:], Dtmp[:], ALU.add), reads=[r_Dtmp, r_MTg[p3]], writes=[r_MTg[p3]]))
            return ops

        def A_units(g):
            p = g % 4
            ti, gl = divmod(g, 8)
            tp = ti % 2
            units = []

            def uLA(half):
                b = 2 + half
                s.do("pe", lambda pe: pe.matmul(PS[b][:], bp[p][:], Dall[p][:, half * 4:half * 4 + 4, :].rearrange("p a b -> p (a b)"), start=True, stop=True),
                     reads=[r_bp[p], r_Dall[p]], writes=[PSR[b]])
                s.do("act", lambda e: e.copy(LAg[p][:, half * 4:half * 4 + 4, :].rearrange("p a b -> p (a b)"), PS[b][:]), reads=[PSR[b]], writes=[r_LAg[p]])

            def uAB(half):
                b = 4 + half

                def mm(pe):
                    inst = None
                    for t4 in range(4):
                        inst = pe.matmul(PS[b][:, t4 * 128:(t4 + 1) * 128], Dall[p][:, half * 4 + t4, :], bp[p][:], start=True, stop=True)
                    return inst
                s.do("pe", mm, reads=[r_bp[p], r_Dall[p]], writes=[PSR[b]])
                s.do("act", lambda e: e.copy(AB[:, half * 4:half * 4 + 4, :].rearrange("p a b -> p (a b)"), PS[b][:]), reads=[PSR[b]], writes=[r_AB])

            def uOCm1():
                b = 2
                s.do("pe", lambda pe: pe.matmul(PS[b][:, 0:128], D2all[p][:, 0, :], cp[p][:], start=True, stop=True), reads=[r_cp[p], r_D2all[p]], writes=[PSR[b]])
                s.do("act", lambda e: e.copy(OCm1[:], PS[b][:, 0:128]), reads=[PSR[b]], writes=[r_OCm1])

            def uOC(half):
                b = 3 + 2 * half

                def mm(pe):
                    inst = None
                    for t4 in range(4):
                        inst = pe.matmul(PS[b][:, t4 * 128:(t4 + 1) * 128], D2all[p][:, 1 + half * 4 + t4, :], cp[p][:], start=True, stop=True)
                    return inst
                s.do("pe", mm, reads=[r_cp[p], r_D2all[p]], writes=[PSR[b]])
                s.do("act", lambda e: e.copy(OCt[tp][:, gl, half * 4:half * 4 + 4, :].rearrange("p a b -> p (a b)"), PS[b][:]), reads=[PSR[b]], writes=[r_OCt[tp][gl]])

            def uT_(half):
                bT = 6 + half

                def mm(pe):
                    inst = None
                    for t4 in range(4):
                        tau = half * 4 + t4
                        inst = pe.matmul(PS[bT][:, t4 * 128:(t4 + 1) * 128], AB[:, tau, :], OCm1[:], start=(gl == 0 and t4 == 0), stop=(gl == 7), skip_group_check=True)
                    return inst
                s.do("pe", mm, reads=[r_AB, r_OCm1], writes=[PSR[bT]])

            units = [lambda: uLA(0), lambda: uLA(1), lambda: uAB(0), lambda: uAB(1), uOCm1, lambda: uOC(0), lambda: uOC(1),
                     lambda: uT_(0), lambda: uT_(1)]
            return units

        def T_evac(ti):
            tp = ti % 2
            s.do("dve", lambda e: e.scalar_tensor_tensor(Tm[tp][:, 0, :], ident[:], dsk[:, ti:ti + 1], PS[6][:, 0:128], ALU.mult, ALU.add),
                 reads=[PSR[6], r_co, r_const], writes=[r_Tm[tp]])
            s.do("dve", lambda e: e.tensor_copy(Tm[tp][:, 1:4, :].rearrange("p a b -> p (a b)"), PS[6][:, 128:512]), reads=[PSR[6]], writes=[r_Tm[tp]])
            s.do("act", lambda e: e.copy(Tm[tp][:, 4:8, :].rearrange("p a b -> p (a b)"), PS[7][:]), reads=[PSR[7]], writes=[r_Tm[tp]])

        def load_u(ti):
            tp = ti % 2
            ld("sp", uT[tp][:], projT[OFF_U + ti * 128:OFF_U + (ti + 1) * 128, :], [r_uT[tp]])

        def output_stage(ti):
            tp = ti % 2
            u3 = uT[tp][:].rearrange("p (c j) -> p c j", j=8)
            y3 = yT[tp][:].rearrange("p (c j) -> p c j", j=8)
            for i in range(8):
                b = 2 + (i % 4)

                def mmY(pe, b=b, i=i):
                    inst = None
                    for j in range(i + 1):
                        inst = pe.matmul(PS[b][:], Tm[tp][:, i - j, :], u3[:, :, j], start=(j == 0), stop=(j == i))
                    for gl in range(8):
                        inst = pe.matmul(PS[b][:, 1:512], OCt[tp][:, gl, i, :], Sb[tp][:, gl, 0:511], start=False, stop=(gl == 7), skip_group_check=True)
                    return inst
                s.do("pe", mmY, reads=[r_Tm[tp], r_uT[tp]] + r_OCt[tp] + r_Sb[tp], writes=[PSR[b]])
                s.do("act", lambda e, b=b, i=i: e.activation(out=y3[:, :, i], in_=PS[b][:], func=AF.Gelu), reads=[PSR[b]], writes=[r_yT[tp]])
            s.dma("sp", lambda e: e.dma_start(out=gyT[ti * 128:(ti + 1) * 128, :], in_=yT[tp][:]), reads=[r_yT[tp]])

        load_u(0)
        for gg_ in range(4):
            for o in A_build(gg_):
                o()
        for gg_ in range(2):
            for u in A_units(gg_):
                u()
        for P in range(32):
            gs = (2 * P, 2 * P + 1)
            ti = gs[0] // 8
            tp = ti % 2
            if gs[0] % 8 == 0 and ti + 1 < 8:
                load_u(ti + 1)
            nxt = (A_units(gs[0] + 2) + A_units(gs[1] + 2)) if P + 1 < 32 else []
            bld = (A_build(gs[0] + 4) + A_build(gs[1] + 4)) if P + 2 < 32 else []
            u3 = uT[tp][:].rearrange("p (c j) -> p c j", j=8)
            for g in gs:
                b = g % 2
                p4 = g % 4

                def mmZ(pe, b=b, p4=p4, u3=u3):
                    inst = None
                    for j in range(8):
                        inst = pe.matmul(PS[b][:], LAg[p4][:, 7 - j, :], u3[:, :, j], start=(j == 0), stop=(j == 7))
                    return inst
                s.do("pe", mmZ, reads=[r_LAg[p4], r_uT[tp]], writes=[PSR[b]])
                s.do("act", lambda e, b=b: e.copy(S32g[b][:], PS[b][:]), reads=[PSR[b]], writes=[r_S32g[b]])
            for k in range(9):
                sh = 2 ** k
                for g in gs:
                    b = g % 2
                    p6 = g % 6
                    s.do("pe", lambda pe, b=b, p6=p6, k=k, sh=sh: pe.matmul(PS[b][:, sh:512], MTg[p6][:, k, :], S32g[b][:, 0:512 - sh], start=True, stop=True),
                         reads=[r_MTg[p6], r_S32g[b]], writes=[PSR[b]])
                for g in gs:
                    b = g % 2
                    s.do("dve", lambda e, b=b, sh=sh: e.tensor_tensor(S32g[b][:, sh:512], S32g[b][:, sh:512], PS[b][:, sh:512], ALU.add),
                         reads=[PSR[b], r_S32g[b]], writes=[r_S32g[b]])
                for q in (2 * k, 2 * k + 1):
                    if q < len(nxt):
                        nxt[q]()
                    if q < len(bld):
                        bld[q]()
            for g in gs:
                b = g % 2
                gl = g % 8
                s.do("act", lambda e, b=b, tp=tp, gl=gl: e.copy(Sb[tp][:, gl, :], S32g[b][:]), reads=[r_S32g[b]], writes=[r_Sb[tp][gl]])
            if P % 4 == 2:
                T_evac(ti)
            if P % 4 == 3:
                output_stage(ti)
        s.barrier()


    def kcview(ap_rows):
        return ap_rows.rearrange("(kc p) t -> p kc t", p=128)

    nb_mod[0] = 8
    if upto == 2:
        es_glob.close()
        raise _Stop(nc)
    with ExitStack() as es:
        wg = sb(es, "wg", [128, 8, 2 * D], BF16)
        wbs = sb(es, "wbs", [128, 8, D], BF16)
        r_wg = [Res() for _ in range(8)]
        r_wbs = [Res() for _ in range(8)]
        for kc in range(8):
            ld("pool", wg[:, kc, :], w_glu[kc * 128:(kc + 1) * 128, :], [r_wg[kc]])
        for kc in range(8):
            ld("pool", wbs[:, kc, :], w_brs[kc * 128:(kc + 1) * 128, :], [r_wbs[kc]])
        bglu = sb(es, "bglu", [128, 16], F32)
        r_bglu = Res()
        ld("sp", bglu[:], b_gluT, [r_bglu])
        gy = [sb(es, "gy%d" % i, [128, 8, TB], BF16) for i in range(2)]
        zs = [sb(es, "zs%d" % i, [128, 8, TB], BF16) for i in range(2)]
        gls = [sb(es, "gls%d" % i, [128, 8, TB], BF16) for i in range(2)]
        r_gy = [Res() for _ in range(2)]
        r_zs = [Res() for _ in range(2)]
        r_gls = [Res() for _ in range(2)]
        g1 = sb(es, "g1", [128, 8, TB], BF16)
        r_g1 = Res()
        mst = [sb(es, "mst%d" % i, [128, 8, TB], BF16) for i in range(2)]
        r_mst = [Res() for _ in range(2)]
        f1 = [sb(es, "f1_%d" % i, [128, TB], F32) for i in range(2)]
        f2 = [sb(es, "f2_%d" % i, [128, TB], F32) for i in range(2)]
        f3 = [sb(es, "f3_%d" % i, [128, TB], F32) for i in range(2)]
        r_f1 = [Res() for _ in range(2)]
        r_f2 = [Res() for _ in range(2)]
        r_f3 = [Res() for _ in range(2)]
        it = 0
        def load3(tbb):
            pp = tbb % 2
            c0_ = tbb * TB
            ld("sp", gy[pp][:], kcview(gyT)[:, :, c0_:c0_ + TB], [r_gy[pp]])
            ld("sp", zs[pp][:], kcview(projT[OFF_ZS:OFF_ZS + D, :])[:, :, c0_:c0_ + TB], [r_zs[pp]])
            ld("sp", gls[pp][:], kcview(projT[OFF_GLS:OFF_GLS + D, :])[:, :, c0_:c0_ + TB], [r_gls[pp]])
        load3(0)
        for tb in range(NTB):
            t0 = tb * TB
            p = tb % 2
            if tb + 1 < NTB:
                load3(tb + 1)
            for oc in range(8):
                ba = nextbank()
                bb = nextbank()
                q = it % 2
                it += 1

                def mmg(pe, ba=ba, bb=bb, oc=oc, p=p):
                    inst = None
                    for kc in range(8):
                        inst = pe.matmul(PS[ba][:], wg[:, kc, oc * 128:(oc + 1) * 128], gy[p][:, kc, :], start=(kc == 0), stop=(kc == 7))
                    for kc in range(8):
                        inst = pe.matmul(PS[bb][:], wg[:, kc, D + oc * 128:D + (oc + 1) * 128], gy[p][:, kc, :], start=(kc == 0), stop=(kc == 7))
                    return inst
                s.do("pe", mmg, reads=r_wg + [r_gy[p]], writes=[PSR[ba], PSR[bb]])
                s.do("act", lambda e, bb=bb, oc=oc, q=q: e.activation(out=f1[q][:], in_=PS[bb][:], func=AF.Sigmoid, bias=bglu[:, 8 + oc:9 + oc], scale=1.0),
                     reads=[PSR[bb], r_bglu], writes=[r_f1[q]])
                s.do("dve", lambda e, ba=ba, oc=oc, q=q: e.scalar_tensor_tensor(f2[q][:], PS[ba][:], bglu[:, oc:oc + 1], f1[q][:], ALU.add, ALU.mult),
                     reads=[PSR[ba], r_bglu, r_f1[q]], writes=[r_f2[q]])
                s.do("dve", lambda e, oc=oc, q=q, p=p: e.tensor_tensor(g1[:, oc, :], f2[q][:], zs[p][:, oc, :], ALU.mult), reads=[r_zs[p], r_f2[q]], writes=[r_g1])
            for oc in range(8):
                b = nextbank()
                q = it % 2
                it += 1

                def mmb(pe, b=b, oc=oc):
                    inst = None
                    for kc in range(8):
                        inst = pe.matmul(PS[b][:], wbs[:, kc, oc * 128:(oc + 1) * 128], g1[:, kc, :], start=(kc == 0), stop=(kc == 7))
                    return inst
                s.do("pe", mmb, reads=r_wbs + [r_g1], writes=[PSR[b]])
                s.do("dve", lambda e, b=b, oc=oc, q=q, p=p: e.tensor_tensor(mst[p][:, oc, :], PS[b][:], gls[p][:, oc, :], ALU.mult), reads=[PSR[b], r_gls[p]], writes=[r_mst[p]])
            s.dma("sp", lambda e, p=p, t0=t0: e.dma_start(out=kcview(msT)[:, :, t0:t0 + TB], in_=mst[p][:]), reads=[r_mst[p]])
        s.barrier()

    if upto == 3:
        es_glob.close()
        raise _Stop(nc)
    nb_mod[0] = 6
    with ExitStack() as es:
        wq = sb(es, "wq", [128, 2, 1536], BF16)
        wkv = sb(es, "wkv", [128, 2, 2048], BF16)
        r_w = Res()
        r_w2 = Res()
        Rm = sb(es, "Rm", [128, 96], BF16)
        invf = sb(es, "invf", [128, 1], F32)
        gq = sb(es, "gq", [128, 2], F32)
        gkv = sb(es, "gkv", [128, 2], F32)
        r_sm = [Res() for _ in range(6)]
        ld("pool", wq[:, 0, :], w_q[0:128, :], [r_w])
        ld("pool", wq[:, 1, :], w_q[128:256, :], [r_w2])
        ld("pool", wkv[:, 0, :], w_kv[0:128, :], [r_sm[0]])
        ld("pool", wkv[:, 1, :], w_kv[128:256, :], [r_sm[1]])
        ld("pool", Rm[:], cR, [r_sm[2]])
        ld("sp", invf[:], cInvf, [r_sm[3]])
        ld("sp", gq[:], g_qT, [r_sm[4]])
        ld("sp", gkv[:], g_kvT, [r_sm[5]])
        r_wall = [r_w, r_w2] + r_sm
        qn = sb(es, "qn", [128, 2, S], BF16)
        kvn = sb(es, "kvn", [128, 2, S], BF16)
        r_qn = [Res() for _ in range(NTB)]
        r_kvn = [Res() for _ in range(NTB)]
        cosT = sb(es, "cosT", [96, S], F32)
        sinT = sb(es, "sinT", [96, S], F32)
        r_trig = Res()
        KR = sb(es, "KR", [96, S], BF16)
        r_KR = Res()
        with ExitStack() as es2:
            posI = sb(es2, "posI", [96, S], I32)
            angT = sb(es2, "angT", [96, S], F32)
            tmpT = sb(es2, "tmpT", [96, S], F32)
            tmpI = sb(es2, "tmpI", [96, S], I32)
            r_tr = Res()
            ld("sp", posI[64:96, :], pos32, [r_tr])
            R = slice(64, 96)

            def V2(fn):
                s.do("dve", fn, reads=[r_tr, r_sm[3]], writes=[r_tr])

            V2(lambda e: e.tensor_copy(angT[R, :], posI[R, :]))
            V2(lambda e: e.tensor_scalar(angT[R, :], angT[R, :], invf[R, 0:1], 0.0, ALU.mult, ALU.add))

            def sin_big(out_t, ang_t):
                V2(lambda e: e.tensor_scalar(tmpT[R, :], ang_t[R, :], 1.0 / TWO_PI, 0.0, ALU.mult, ALU.add))
                V2(lambda e: e.tensor_copy(tmpI[R, :], tmpT[R, :]))
                V2(lambda e: e.tensor_copy(tmpT[R, :], tmpI[R, :]))
                V2(lambda e: e.scalar_tensor_tensor(out_t[R, :], tmpT[R, :], -C1, ang_t[R, :], ALU.mult, ALU.add))
                V2(lambda e: e.scalar_tensor_tensor(out_t[R, :], tmpT[R, :], -C2, out_t[R, :], ALU.mult, ALU.add))
                V2(lambda e: e.tensor_scalar(out_t[R, :], out_t[R, :], SIN_CLAMP, -SIN_CLAMP, ALU.min, ALU.max))
                s.do("act", lambda e: e.activation(out=out_t[R, :], in_=out_t[R, :], func=AF.Sin), reads=[r_tr], writes=[r_tr, r_trig])
            sin_big(sinT, angT)
            V2(lambda e: e.tensor_scalar(angT[R, :], angT[R, :], math.pi / 2, 0.0, ALU.add, ALU.add))
            sin_big(cosT, angT)
            s.barrier()
        ta = sb(es, "ta", [96, TB], F32)
        tb_ = sb(es, "tb_", [96, TB], F32)
        r_ta = Res()
        r_tb = Res()
        R = slice(64, 96)
        with ExitStack() as es3:
            krT = sb(es3, "krT", [96, S], BF16)
            r_krT = Res()
            ld("sp", krT[64:96, :], projT[OFF_KR:OFF_KR + 32, :], [r_krT])
            ql = [sb(es3, "ql%d" % i, [128, 2, TB], BF16) for i in range(2)]
            r_ql = [Res() for _ in range(2)]
            sq2 = sb(es3, "sq2", [128, 2, TB], BF16)
            r_sq2 = Res()
            rs2 = sb(es3, "rs2", [128, TB], F32)
            r_rs2 = Res()
            rt2 = sb(es3, "rt2", [128, TB], F32)
            r_rt2 = Res()
            it = 0
            for (off, dst, rdst, gvec, gi) in ((OFF_Q, qn, r_qn, gq, 4), (OFF_KV, kvn, r_kvn, gkv, 5)):
                for tb in range(NTB):
                    t0 = tb * TB
                    p = it % 2
                    it += 1
                    ld("sp", ql[p][:], kcview(projT[off:off + 256, :])[:, :, t0:t0 + TB], [r_ql[p]])
                    s.do("act", lambda e, p=p: e.activation(out=sq2[:], in_=ql[p][:], func=AF.Square), reads=[r_ql[p]], writes=[r_sq2])
                    b = nextbank()

                    def mmss2(pe, b=b):
                        inst = None
                        for kc in range(2):
                            inst = pe.matmul(PS[b][:], ones_bf[:], sq2[:, kc, :], start=(kc == 0), stop=(kc == 1))
                        return inst
                    s.do("pe", mmss2, reads=[r_sq2, r_const], writes=[PSR[b]])
                    rsqrt_mean(PS[b][:], rs2[:], 256, PSR[b], r_rs2, rt2[:], r_rt2)
                    for kc in range(2):
                        s.do("dve", lambda e, kc=kc, p=p, dst=dst, gvec=gvec, t0=t0: e.scalar_tensor_tensor(dst[:, kc, t0:t0 + TB], ql[p][:, kc, :], gvec[:, kc:kc + 1], rs2[:], ALU.mult, ALU.mult),
                             reads=[r_ql[p], r_rs2, r_sm[gi]], writes=[rdst[tb]])
            for tb in range(NTB):
                t0 = tb * TB
                s.do("pe", lambda pe, t0=t0: pe.matmul(PS[7][0:96, :], Rm[R, 0:96], krT[R, t0:t0 + TB], start=True, stop=True), reads=[r_krT, r_sm[2]], writes=[PSR[7]])
                s.do("dve", lambda e, t0=t0: e.tensor_tensor(ta[R, :], PS[7][R, :], sinT[R, t0:t0 + TB], ALU.mult), reads=[PSR[7], r_trig], writes=[r_ta])
                s.do("dve", lambda e, t0=t0: e.tensor_tensor(tb_[R, :], krT[R, t0:t0 + TB], cosT[R, t0:t0 + TB], ALU.mult), reads=[r_krT, r_trig], writes=[r_tb])
                s.do("dve", lambda e, t0=t0: e.tensor_tensor(KR[R, t0:t0 + TB], ta[R, :], tb_[R, :], ALU.add), reads=[r_ta, r_tb], writes=[r_KR])

            s.barrier()
        KT = sb(es, "KT", [96, 4, S], BF16)
        r_KT = [Res() for _ in range(4)]
        Vt = sb(es, "Vt", [128, 32, 4, 128], BF16)
        r_V = Res()
        qsb = [sb(es, "qsb%d" % i, [96, TB], BF16) for i in range(2)]
        r_qsb = [Res() for _ in range(2)]
        PT = [sb(es, "PT%d" % i, [128, TB], BF16) for i in range(4)]
        r_PT = [Res() for _ in range(4)]
        recs = sb(es, "recs", [128, TB], F32)
        r_recs = Res()
        r_ones = Res()
        s.do("dve", lambda e: e.memset(Vt[:, :, :, 64:128], 1.0), writes=[r_ones])
        zmb = [sb(es, "zmb%d" % i, [64, TB], BF16) for i in range(3)]
        r_zmb = [Res() for _ in range(3)]
        ymst = [sb(es, "ymst%d" % i, [64, TB], BF16) for i in range(2)]
        r_ymst = [Res() for _ in range(2)]
        fa = sb(es, "fa", [64, TB], F32)
        fb = sb(es, "fb", [64, TB], F32)
        fc = sb(es, "fc", [64, TB], F32)
        r_fa, r_fb, r_fc = Res(), Res(), Res()
        wkv_h = wkv[:].rearrange("p k (h c) -> p k h c", c=128)
        ipt = 0
        ihq = 0
        for hg in range(4):
            for hl in range(4):
                h = hg * 4 + hl
                for tb in range(NTB):
                    t0 = tb * TB
                    b = nextbank()

                    def mmk(pe, b=b, h=h, t0=t0):
                        inst = None
                        for kc in range(2):
                            inst = pe.matmul(PS[b][0:64, :], wkv[:, kc, h * 128:h * 128 + 64], kvn[:, kc, t0:t0 + TB], start=(kc == 0), stop=(kc == 1))
                        return inst
                    s.do("pe", mmk, reads=[r_sm[0], r_sm[1], r_kvn[tb]], writes=[PSR[b]])
                    if tb % 2 == 0:
                        s.do("act", lambda e, b=b, hl=hl, t0=t0: e.copy(KT[0:64, hl, t0:t0 + TB], PS[b][0:64, :]), reads=[PSR[b]], writes=[r_KT[hl]])
                    else:
                        s.do("dve", lambda e, b=b, hl=hl, t0=t0: e.tensor_copy(KT[0:64, hl, t0:t0 + TB], PS[b][0:64, :]), reads=[PSR[b]], writes=[r_KT[hl]])
                s.do("pool", lambda e, hl=hl: e.tensor_copy(KT[R, hl, :], KR[R, :]), reads=[r_KR], writes=[r_KT[hl]])
            for kt in range(32):
                b = nextbank()

                def mmv(pe, b=b, kt=kt, hg=hg):
                    inst = None
                    for kc in range(2):
                        inst = pe.matmul(PS[b][:, 0:256].rearrange("p (h c) -> p h c", c=64), kvn[:, kc, kt * 128:(kt + 1) * 128],
                                         wkv_h[:, kc, hg * 4:hg * 4 + 4, 64:128], start=(kc == 0), stop=(kc == 1))
                    return inst
                s.do("pe", mmv, reads=[r_sm[0], r_sm[1], r_kvn[kt // 4]], writes=[PSR[b]])
                if kt % 2 == 0:
                    s.do("act", lambda e, b=b, kt=kt: e.copy(Vt[:, kt, :, 0:64], PS[b][:, 0:256].rearrange("p (h c) -> p h c", c=64)), reads=[PSR[b]], writes=[r_V])
                else:
                    s.do("dve", lambda e, b=b, kt=kt: e.tensor_copy(Vt[:, kt, :, 0:64], PS[b][:, 0:256].rearrange("p (h c) -> p h c", c=64)), reads=[PSR[b]], writes=[r_V])
            items = [(qb, hl) for qb in (7, 0, 6, 1, 5, 2, 4, 3) for hl in range(4)]
            meta = {}

            def prepA(ii):
                qb, hl = items[ii]
                h = hg * 4 + hl
                t0 = qb * TB
                pq = (ihq0 + ii) % 2
                Q = qsb[pq]
                rQ = r_qsb[pq]
                par = (ihq0 + ii) % 2
                pz = (ihq0 + ii) % 3
                meta[ii] = dict(qb=qb, hl=hl, h=h, t0=t0, pq=pq, pz=pz, Q=Q, rQ=rQ, bo=3 + par, nkt=4 * qb + 4)
                ld("sp", zmb[pz][:], projT[OFF_ZM + h * 64:OFF_ZM + (h + 1) * 64, t0:t0 + TB], [r_zmb[pz]])

                def mmq(pe):
                    inst = None
                    for kc in range(2):
                        inst = pe.matmul(PS[5][0:96, :], wq[:, kc, h * 96:(h + 1) * 96], qn[:, kc, t0:t0 + TB], start=(kc == 0), stop=(kc == 1))
                    return inst
                s.do("pe", mmq, reads=[r_w, r_w2, r_qn[qb]], writes=[PSR[5]])
                s.do("act", lambda e: e.copy(Q[:], PS[5][0:96, :]), reads=[PSR[5]], writes=[rQ])

            def prepB(ii):
                m = meta[ii]
                Q, rQ, t0 = m["Q"], m["rQ"], m["t0"]
                s.do("pe", lambda pe: pe.matmul(PS[6][0:96, :], Rm[R, 0:96], Q[R, :], start=True, stop=True), reads=[rQ, r_sm[2]], writes=[PSR[6]])
                s.do("dve", lambda e: e.tensor_tensor(ta[R, :], PS[6][R, :], sinT[R, t0:t0 + TB], ALU.mult), reads=[PSR[6], r_trig], writes=[r_ta])
                s.do("dve", lambda e: e.tensor_tensor(tb_[R, :], PS[5][R, :], cosT[R, t0:t0 + TB], ALU.mult), reads=[PSR[5], r_trig], writes=[r_tb])
                s.do("pool", lambda e: e.tensor_tensor(Q[R, :], ta[R, :], tb_[R, :], ALU.add), reads=[r_ta, r_tb], writes=[rQ])

            flat = []
            for ii, (qb, hl) in enumerate(items):
                for kt in range(4 * qb + 4):
                    flat.append((ii, kt))
            NF = len(flat)
            ptof = {}

            def issueS(n):
                ii, kt = flat[n]
                m = meta[ii]
                a = kt - 4 * m["qb"]
                c0 = 128 * a if a >= 0 else 0
                bs = n % 3
                Q, hl = m["Q"], m["hl"]
                s.do("pe", lambda pe: pe.matmul(PS[bs][:, c0:TB], KT[0:96, hl, kt * 128:(kt + 1) * 128], Q[0:96, c0:TB], start=True, stop=True),
                     reads=[r_KT[hl], m["rQ"]], writes=[PSR[bs]])

            def issueExp(n):
                ii, kt = flat[n]
                m = meta[ii]
                a = kt - 4 * m["qb"]
                c0 = 128 * a if a >= 0 else 0
                bs = n % 3
                pt = n % 4
                s.do("act", lambda e: e.activation(out=PT[pt][:, c0:TB], in_=PS[bs][:, c0:TB], func=AF.Exp, scale=ATT_SCALE),
                     reads=[PSR[bs]], writes=[r_PT[pt]])
                if a >= 0:
                    s.do("pool", lambda e: e.memset(PT[pt][64:128, c0:c0 + 64], 0.0), reads=[r_PT[pt]], writes=[r_PT[pt]])

            def issuePV(n):
                ii, kt = flat[n]
                m = meta[ii]
                a = kt - 4 * m["qb"]
                c0 = 128 * a if a >= 0 else 0
                pt = n % 4
                bo, nkt, hl = m["bo"], m["nkt"], m["hl"]
                def mmpv(pe):
                    if SPLIT_PV:
                        pe.matmul(PS[bo][0:64, c0:TB], Vt[:, kt, hl, 0:64], PT[pt][:, c0:TB], start=(kt == 0), stop=(kt == nkt - 1), skip_group_check=True)
                        return pe.matmul(PS[bo][64:128, c0:TB], Vt[:, kt, hl, 64:128], PT[pt][:, c0:TB], start=(kt == 0), stop=(kt == nkt - 1), skip_group_check=True)
                    return pe.matmul(PS[bo][:, c0:TB], Vt[:, kt, hl, :], PT[pt][:, c0:TB], start=(kt == 0), stop=(kt == nkt - 1), skip_group_check=True)
                s.do("pe", mmpv, reads=[r_V, r_ones, r_PT[pt]], writes=[PSR[bo]])

            def finalizeA(ii):
                m = meta[ii]
                bo = m["bo"]
                s.do("dve", lambda e: e.reciprocal(recs[64:128, :], PS[bo][64:128, :]), reads=[PSR[bo]], writes=[r_recs])

            def finalizeB(ii):
                m = meta[ii]
                pq, pz, bo, h, t0 = m["pq"], m["pz"], m["bo"], m["h"], m["t0"]
                zm = zmb[pz]
                s.do("pe", lambda pe: pe.matmul(PS[7][0:64, :], ident[64:128, 64:128], recs[64:128, :], start=True, stop=True), reads=[r_recs, r_const], writes=[PSR[7]])
                s.do("act", lambda e: e.copy(fa[:], PS[7][0:64, :]), reads=[PSR[7]], writes=[r_fa])
                s.do("pool", lambda e: e.tensor_tensor(fc[:], fa[:], zm[:], ALU.mult), reads=[r_fa, r_zmb[pz]], writes=[r_fc])
                s.do("dve", lambda e: e.tensor_tensor(ymst[pq][:], PS[bo][0:64, :], fc[:], ALU.mult), reads=[PSR[bo], r_fc], writes=[r_ymst[pq]])
                s.dma("sp", lambda e: e.dma_start(out=ymT[h * 64:(h + 1) * 64, t0:t0 + TB], in_=ymst[pq][:]), reads=[r_ymst[pq]])

            ihq0 = hg * len(items)
            prepA(0)
            prepB(0)
            issueS(0)
            issueS(1)
            for n in range(NF):
                ii, kt = flat[n]
                if kt == 0 and ii + 1 < len(items):
                    prepA(ii + 1)
                if kt == 1 and ii + 1 < len(items):
                    prepB(ii + 1)
                if kt == 1 and ii >= 1:
                    finalizeA(ii - 1)
                if kt == min(5, meta[ii]["nkt"] - 1) and ii >= 1:
                    finalizeB(ii - 1)
                if n + 2 < NF:
                    issueS(n + 2)
                issueExp(n)
                issuePV(n)
                if FILLER_N > 0:
                    s.do("pe", lambda pe: pe.matmul(PS[7][:, 0:FILLER_N], ones_bf[:], ones_bf[:].unsqueeze(1).broadcast_to([128, FILLER_N // 128, 128]).rearrange("p a b -> p (a b)") if False else PT[0][:, 0:FILLER_N], start=True, stop=True))


            finalizeA(len(items) - 1)
            finalizeB(len(items) - 1)
        s.barrier()

    nb_mod[0] = 8
    if upto == 4:
        es_glob.close()
        raise _Stop(nc)
    with ExitStack() as es:
        wbm = sb(es, "wbm", [128, 8, D], BF16)
        wo = sb(es, "wo", [128, 8, D], BF16)
        r_wbm = [Res() for _ in range(8)]
        r_wo = [Res() for _ in range(8)]
        for kc in range(8):
            ld("pool", wbm[:, kc, :], w_brm[kc * 128:(kc + 1) * 128, :], [r_wbm[kc]])
        for kc in range(8):
            ld("pool", wo[:, kc, :], w_o[kc * 128:(kc + 1) * 128, :], [r_wo[kc]])
        ym = [sb(es, "ym%d" % i, [128, 8, TB], BF16) for i in range(2)]
        glm = [sb(es, "glm%d" % i, [128, 8, TB], BF16) for i in range(2)]
        msb = [sb(es, "msb%d" % i, [128, 8, TB], BF16) for i in range(2)]
        xb2 = [sb(es, "xb2_%d" % i, [128, 8, TB], F32) for i in range(2)]
        r_ym = [Res() for _ in range(2)]
        r_glm = [Res() for _ in range(2)]
        r_msb = [Res() for _ in range(2)]
        r_xb2 = [Res() for _ in range(2)]
        mg = sb(es, "mg", [128, 8, TB], BF16)
        r_mg = Res()
        o32 = sb(es, "o32", [128, 8, TB], F32)
        r_o32 = Res()
        sq6 = sb(es, "sq6", [128, 8, TB], BF16)
        r_sq6 = Res()
        h1 = [sb(es, "h1_%d" % i, [128, TB], F32) for i in range(2)]
        h2 = [sb(es, "h2_%d" % i, [128, TB], F32) for i in range(2)]
        r_h1 = [Res() for _ in range(2)]
        r_h2 = [Res() for _ in range(2)]
        rs6 = sb(es, "rs6", [128, TB], F32)
        rt6 = sb(es, "rt6", [128, TB], F32)
        r_rs6, r_rt6 = Res(), Res()
        it = 0
        def load6(tbb):
            pp = tbb % 2
            c0_ = tbb * TB
            ld("sp", ym[pp][:], kcview(ymT)[:, :, c0_:c0_ + TB], [r_ym[pp]])
            ld("sp", glm[pp][:], kcview(projT[OFF_GLM:OFF_GLM + D, :])[:, :, c0_:c0_ + TB], [r_glm[pp]])
            ld("sp", msb[pp][:], kcview(msT)[:, :, c0_:c0_ + TB], [r_msb[pp]])
            ld("sp", xb2[pp][:], kcview(xT)[:, :, c0_:c0_ + TB], [r_xb2[pp]])
        load6(0)
        for tb in range(NTB):
            t0 = tb * TB
            p = tb % 2
            if tb + 1 < NTB:
                load6(tb + 1)
            for oc in range(8):
                b = nextbank()
                q = it % 2
                it += 1

                def mm1(pe, b=b, oc=oc, p=p):
                    inst = None
                    for kc in range(8):
                        inst = pe.matmul(PS[b][:], wbm[:, kc, oc * 128:(oc + 1) * 128], ym[p][:, kc, :], start=(kc == 0), stop=(kc == 7))
                    return inst
                s.do("pe", mm1, reads=r_wbm + [r_ym[p]], writes=[PSR[b]])
                s.do("dve", lambda e, b=b, q=q, oc=oc, p=p: e.tensor_tensor(h2[q][:], PS[b][:], glm[p][:, oc, :], ALU.mult), reads=[PSR[b], r_glm[p]], writes=[r_h2[q]])
                s.do("pool", lambda e, oc=oc, q=q, p=p: e.tensor_tensor(mg[:, oc, :], h2[q][:], msb[p][:, oc, :], ALU.add), reads=[r_h2[q], r_msb[p]], writes=[r_mg])
            for oc in range(8):
                b = nextbank()

                def mm2(pe, b=b, oc=oc):
                    inst = None
                    for kc in range(8):
                        inst = pe.matmul(PS[b][:], wo[:, kc, oc * 128:(oc + 1) * 128], mg[:, kc, :], start=(kc == 0), stop=(kc == 7))
                    return inst
                s.do("pe", mm2, reads=r_wo + [r_mg], writes=[PSR[b]])
                s.do("act", lambda e, b=b, oc=oc: e.copy(o32[:, oc, :], PS[b][:]), reads=[PSR[b]], writes=[r_o32])
                s.do("act", lambda e, b=b, oc=oc: e.activation(out=sq6[:, oc, :], in_=PS[b][:], func=AF.Square), reads=[PSR[b]], writes=[r_sq6])
            b = nextbank()

            def mmss6(pe, b=b):
                inst = None
                for kc in range(8):
                    inst = pe.matmul(PS[b][:], ones_bf[:], sq6[:, kc, :], start=(kc == 0), stop=(kc == 7))
                return inst
            s.do("pe", mmss6, reads=[r_sq6, r_const], writes=[PSR[b]])
            rsqrt_mean(PS[b][:], rs6[:], D, PSR[b], r_rs6, rt6[:], r_rt6)
            for oc in range(8):
                s.do("dve", lambda e, oc=oc: e.scalar_tensor_tensor(o32[:, oc, :], o32[:, oc, :], gg[:, oc:oc + 1], rs6[:], ALU.mult, ALU.mult),
                     reads=[r_o32, r_rs6, r_mod], writes=[r_o32])
                s.do("dve", lambda e, oc=oc, p=p: e.tensor_tensor(xb2[p][:, oc, :], o32[:, oc, :], xb2[p][:, oc, :], ALU.add),
                     reads=[r_o32, r_xb2[p]], writes=[r_xb2[p]])
            s.dma("sp", lambda e, p=p, t0=t0: e.dma_start(out=kcview(outT)[:, :, t0:t0 + TB], in_=xb2[p][:]), reads=[r_xb2[p]])
        s.barrier()
    es_glob.close()
    return nc


def _prep_shared(inp):
    f = np.float32
    sh = {}
    sh["w_ada"] = np.ascontiguousarray(inp["w_ada"][0], f)
    sh["b_adaT"] = np.ascontiguousarray(inp["b_ada"][0].reshape(24, 128).T, f)
    sh["g_preT"] = np.ascontiguousarray(inp["g_pre"][0].reshape(8, 128).T, f)
    sh["w_in"] = np.ascontiguousarray(inp["w_in"][0], f)
    a_re = np.asarray(inp["ssm_a_re"][0], f)
    a_im = np.asarray(inp["ssm_a_im"][0], f)
    sh["lrT"] = np.ascontiguousarray(np.concatenate([a_re.T, a_re.T], 0), f)
    sh["liT"] = np.ascontiguousarray(np.concatenate([a_im.T, a_im.T], 0), f)
    sh["ldt"] = np.ascontiguousarray(np.tile(np.asarray(inp["ssm_log_dt"][0], f)[None, :], (128, 1)), f)
    b_re = np.asarray(inp["ssm_b_re"][0], f)
    b_im = np.asarray(inp["ssm_b_im"][0], f)
    c_re = np.asarray(inp["ssm_c_re"][0], f)
    c_im = np.asarray(inp["ssm_c_im"][0], f)
    Bpad = np.zeros((64, 128, 128), f)
    Cpad = np.zeros((64, 128, 128), f)
    for g in range(64):
        gl = g % 8
        Bpad[g, 0:64, 16 * gl:16 * gl + 16] = b_re[g]
        Bpad[g, 64:128, 16 * gl:16 * gl + 16] = b_im[g]
        Cpad[g, 0:64, 16 * gl:16 * gl + 16] = c_re[g].T
        Cpad[g, 64:128, 16 * gl:16 * gl + 16] = c_im[g].T
    sh["Bpad"] = Bpad
    sh["Cpad"] = Cpad
    sh["dT"] = np.ascontiguousarray(np.asarray(inp["ssm_d"][0], f).reshape(8, 128).T, f)
    sh["w_glu"] = np.ascontiguousarray(inp["w_glu"][0], f)
    sh["b_gluT"] = np.ascontiguousarray(inp["b_glu"][0].reshape(16, 128).T, f)
    sh["g_qT"] = np.ascontiguousarray(inp["g_q_norm"][0].reshape(2, 128).T, f)
    sh["w_q"] = np.ascontiguousarray(inp["w_q_up"][0], f)
    sh["g_kvT"] = np.ascontiguousarray(inp["g_kv_norm"][0].reshape(2, 128).T, f)
    sh["w_kv"] = np.ascontiguousarray(inp["w_kv_up"][0], f)
    sh["w_brs"] = np.ascontiguousarray(inp["w_br_ssm"][0], f)
    sh["w_brm"] = np.ascontiguousarray(inp["w_br_mla"][0], f)
    sh["w_o"] = np.ascontiguousarray(inp["w_out"][0], f)
    sh["g_postT"] = np.ascontiguousarray(inp["g_post"][0].reshape(8, 128).T, f)
    sh["cI"] = np.eye(128, dtype=f)
    P = np.zeros((128, 128), f)
    for n in range(64):
        P[n, 64 + n] = 1.0
        P[64 + n, n] = 1.0
    sh["cP"] = P
    sh["cSgn"] = np.concatenate([np.ones(64, f), -np.ones(64, f)])[:, None]
    Rb = np.zeros((128, 96), f)
    for i in range(16):
        Rb[64 + 16 + i, 64 + i] = -1.0
        Rb[64 + i, 64 + 16 + i] = 1.0
    sh["cR"] = Rb
    inv = (10000.0 ** (-np.arange(0, 32, 2, dtype=np.float32) / np.float32(32))).astype(f)
    iv = np.zeros((128, 1), f)
    iv[64:80, 0] = inv
    iv[80:96, 0] = inv
    sh["cInvf"] = iv
    return sh


_NC_CACHE = {}


def kernel(**inputs):
    inp = {k: np.asarray(v) for k, v in inputs.items()}
    sh = _prep_shared(inp)
    x = np.asarray(inp["x"], np.float32)
    c = np.asarray(inp["c"], np.float32)
    pos = np.asarray(inp["positions"], np.int32)
    in_maps = []
    for b in range(8):
        m = dict(sh)
        m["xT"] = np.ascontiguousarray(x[b].T)
        m["cT"] = np.ascontiguousarray(c[b].reshape(8, 128).T)
        m["pos32"] = np.ascontiguousarray(np.tile(pos[b][None, :], (32, 1)))
        in_maps.append(m)
    if "nc" not in _NC_CACHE:
        _NC_CACHE["nc"] = build(DEBUG_OUT)
    nc = _NC_CACHE["nc"]
    res = run_bass_kernel_spmd(nc, in_maps, core_ids=list(range(8)))
    _NC_CACHE["last"] = res
    out = np.stack([np.ascontiguousarray(res.results[b]["outT"].T) for b in range(8)], 0)
    return out.astype(np.float32)
```

```python
import math
from contextlib import ExitStack
import numpy as np
import concourse.bass as bass
import concourse.mybir as mybir
from concourse.bass_utils import run_bass_kernel_spmd

F32 = mybir.dt.float32
BF16 = mybir.dt.bfloat16
I32 = mybir.dt.int32
ALU = mybir.AluOpType
AF = mybir.ActivationFunctionType

D = 1024
S = 4096
NTB = 8
TB = 512
INW = 5664
EPS = 1e-6
OFF_U, OFF_ZS, OFF_Q, OFF_KV, OFF_KR, OFF_ZM, OFF_GLS, OFF_GLM = 0, 1024, 2048, 2304, 2560, 2592, 3616, 4640
ATT_SCALE = 96 ** -0.5
TWO_PI = 2.0 * math.pi
C1 = 6.28125
C2 = TWO_PI - C1
SIN_CLAMP = 3.141592
SAME_ENGINE_SYNC = True
SEM_LIMIT = 30000
NDS = 40

DEBUG_OUT = False
import os
L1MODE = int(os.environ.get('L1MODE', '0'))
SPLIT_PV = False
FILLER_N = int(os.environ.get('FILLER_N', '0'))


class Res:
    __slots__ = ("name", "w", "r")

    def __init__(self, name=""):
        self.name = name
        self.w = None
        self.r = []


class Sched:
    def __init__(self, nc):
        self.nc = nc
        self.E = {"pe": nc.tensor, "act": nc.scalar, "dve": nc.vector, "pool": nc.gpsimd, "sp": nc.sync}
        self.gen = {e: 0 for e in self.E}
        self.sem = {e: nc.alloc_semaphore("s_%s_0" % e) for e in self.E}
        self.cnt = {e: 0 for e in self.E}
        self.waited = {e: {} for e in self.E}
        self.dsems = [nc.alloc_semaphore("d%d" % i) for i in range(NDS)]
        self.dcnt = [0] * NDS
        self.dq = {"sp": list(range(0, 16)), "pool": list(range(NDS - 10, NDS))}
        self.dnext = {"sp": 0, "pool": 0}
        self.old = []
        self.nops = 0

    def _wait(self, e, tok):
        key, sem, val = tok
        if self.waited[e].get(key, 0) >= val:
            return
        self.E[e].wait_ge(sem, val)
        self.waited[e][key] = val

    def _collect(self, e, reads, writes):
        toks = []
        for r in reads:
            if r.w is not None:
                toks.append(r.w)
        for w in writes:
            if w.w is not None:
                toks.append(w.w)
            toks.extend(w.r)
        for tok in toks:
            key = tok[0]
            if isinstance(key, tuple) and key[0] == e:
                if e in ("pe", "sp") or not SAME_ENGINE_SYNC:
                    continue
            self._wait(e, tok)

    def _mark(self, tok, reads, writes):
        for r in reads:
            r.r.append(tok)
        for w in writes:
            w.w = tok
            w.r = []

    def do(self, e, fn, reads=(), writes=()):
        self._collect(e, reads, writes)
        if self.cnt[e] >= SEM_LIMIT:
            self.old.append(((e, self.gen[e]), self.sem[e], self.cnt[e]))
            self.gen[e] += 1
            self.sem[e] = self.nc.alloc_semaphore("s_%s_%d" % (e, self.gen[e]))
            self.cnt[e] = 0
        inst = fn(self.E[e])
        self.cnt[e] += 1
        inst.then_inc(self.sem[e], 1)
        tok = ((e, self.gen[e]), self.sem[e], self.cnt[e])
        self._mark(tok, reads, writes)
        self.nops += 1
        return tok

    def dma(self, q, fn, reads=(), writes=(), n=1):
        self._collect(q, reads, writes)
        lst = self.dq[q]
        i = lst[self.dnext[q] % len(lst)]
        self.dnext[q] += 1
        if self.dcnt[i] > 0:
            self._wait(q, (("d", i), self.dsems[i], self.dcnt[i]))
        insts = fn(self.E[q])
        if not isinstance(insts, (list, tuple)):
            insts = [insts]
        for inst in insts:
            inst.then_inc(self.dsems[i], 16)
            self.dcnt[i] += 16
        tok = (("d", i), self.dsems[i], self.dcnt[i])
        self._mark(tok, reads, writes)
        return tok

    def barrier(self):
        toks = []
        for e in self.E:
            if self.cnt[e] > 0:
                toks.append(((e, self.gen[e]), self.sem[e], self.cnt[e]))
        toks.extend(self.old)
        for i in range(NDS):
            if self.dcnt[i] > 0:
                toks.append((("d", i), self.dsems[i], self.dcnt[i]))
        for e in self.E:
            for tok in toks:
                if tok[0][0] == e:
                    continue
                self._wait(e, tok)


class _Stop(Exception):
    pass


def build(debug=False, upto=99):
    try:
        return _build(debug, upto)
    except _Stop as st:
        return st.args[0]


def _build(debug=False, upto=99):
    nc = bass.Bass("TRN2", target_bir_lowering=False)
    s = Sched(nc)

    def chk(v):
        if upto == v:
            s.barrier()
            raise _Stop(nc)

    def din(name, shape, dt=F32):
        return nc.dram_tensor(name, list(shape), dt, kind="ExternalInput").ap()

    xT = din("xT", [D, S])
    cT = din("cT", [128, 8])
    pos32 = din("pos32", [32, S], I32)
    w_ada = din("w_ada", [D, 3 * D])
    b_adaT = din("b_adaT", [128, 24])
    g_preT = din("g_preT", [128, 8])
    w_in = din("w_in", [D, INW])
    lrT = din("lrT", [128, 64])
    liT = din("liT", [128, 64])
    ldt = din("ldt", [128, 64])
    Bpad = din("Bpad", [64, 128, 128])
    Cpad = din("Cpad", [64, 128, 128])
    dT = din("dT", [128, 8])
    w_glu = din("w_glu", [D, 2 * D])
    b_gluT = din("b_gluT", [128, 16])
    g_qT = din("g_qT", [128, 2])
    w_q = din("w_q", [256, 1536])
    g_kvT = din("g_kvT", [128, 2])
    w_kv = din("w_kv", [256, 2048])
    w_brs = din("w_brs", [D, D])
    w_brm = din("w_brm", [D, D])
    w_o = din("w_o", [D, D])
    g_postT = din("g_postT", [128, 8])
    cI = din("cI", [128, 128])
    cP = din("cP", [128, 128])
    cSgn = din("cSgn", [128, 1])
    cR = din("cR", [128, 96])
    cInvf = din("cInvf", [128, 1])

    outT = nc.dram_tensor("outT", [D, S], F32, kind="ExternalOutput").ap()
    kind_scr = "ExternalOutput" if debug else "Internal"
    projT = nc.dram_tensor("projT", [INW, S], BF16, kind=kind_scr).ap()
    gyT = nc.dram_tensor("gyT", [D, S], BF16, kind=kind_scr).ap()
    msT = nc.dram_tensor("msT", [D, S], BF16, kind=kind_scr).ap()
    ymT = nc.dram_tensor("ymT", [D, S], BF16, kind=kind_scr).ap()

    es_glob = ExitStack()

    def sb(es, name, shape, dt):
        return es.enter_context(nc.sbuf_tensor(name, list(shape), dt))

    PS = [es_glob.enter_context(nc.psum_tensor("ps%d" % i, [128, 512], F32)) for i in range(8)]
    PSR = [Res("ps%d" % i) for i in range(8)]

    ident = sb(es_glob, "ident", [128, 128], F32)
    swapP = sb(es_glob, "swapP", [128, 128], F32)
    sgn = sb(es_glob, "sgn", [128, 1], F32)
    ones_bf = sb(es_glob, "ones_bf", [128, 128], BF16)
    mod = sb(es_glob, "mod", [128, 24], F32)
    gm = sb(es_glob, "gm", [128, 8], F32)
    gg = sb(es_glob, "gg", [128, 8], F32)
    shift_bf = sb(es_glob, "shift_bf", [128, 8], BF16)
    swb = sb(es_glob, "swb", [128, 48], F32)
    r_const = Res("const")
    r_mod = Res("mod")

    def ld(q, out_ap, in_ap, writes, reads=()):
        return s.dma(q, lambda e: e.dma_start(out=out_ap, in_=in_ap), reads=reads, writes=writes)

    ld("sp", ident[:], cI, [r_const])
    ld("sp", swapP[:], cP, [r_const])
    ld("sp", sgn[:], cSgn, [r_const])
    s.do("dve", lambda e: e.memset(ones_bf[:], 1.0), writes=[r_const])

    eps_t = sb(es_glob, "eps_t", [128, 1], F32)
    EPS_AP = eps_t
    s.do("dve", lambda e: e.memset(eps_t[:], EPS), writes=[r_const])
    es_wi = ExitStack()
    wi = sb(es_wi, "wi", [128, 8, INW], BF16)
    r_wi = [Res("wi%d" % k) for k in range(8)]
    for kc in range(8):
        ld("pool", wi[:, kc, :], w_in[kc * 128:(kc + 1) * 128, :], [r_wi[kc]])
    with ExitStack() as es:
        wa = sb(es, "wa", [128, 8, 3 * D], F32)
        c_sb = sb(es, "c_sb", [128, 8], F32)
        bada = sb(es, "bada", [128, 24], F32)
        gpre = sb(es, "gpre", [128, 8], F32)
        gpost = sb(es, "gpost", [128, 8], F32)
        tmp8 = sb(es, "tmp8", [128, 8], F32)
        r_wa = [Res("wa%d" % k) for k in range(8)]
        r_small = Res("small0")
        for kc in range(8):
            ld("sp", wa[:, kc, :], w_ada[kc * 128:(kc + 1) * 128, :], [r_wa[kc]])
        ld("sp", c_sb[:], cT, [r_small])
        ld("sp", bada[:], b_adaT, [r_small])
        ld("sp", gpre[:], g_preT, [r_small])
        ld("sp", gpost[:], g_postT, [r_small])

        def mm0(pe):
            inst = None
            for j in range(24):
                for kc in range(8):
                    inst = pe.matmul(PS[0][:, j:j + 1], wa[:, kc, j * 128:(j + 1) * 128], c_sb[:, kc:kc + 1],
                                     start=(kc == 0), stop=(kc == 7))
            return inst
        s.do("pe", mm0, reads=r_wa + [r_small], writes=[PSR[0]])
        s.do("dve", lambda e: e.tensor_tensor(mod[:], PS[0][:, 0:24], bada[:], ALU.add), reads=[PSR[0], r_small], writes=[r_mod])
        s.do("dve", lambda e: e.tensor_scalar(tmp8[:], mod[:, 8:16], 1.0, 0.0, ALU.add, ALU.add), reads=[r_mod], writes=[r_small])
        s.do("dve", lambda e: e.tensor_tensor(gm[:], tmp8[:], gpre[:], ALU.mult), reads=[r_small], writes=[r_mod])
        s.do("dve", lambda e: e.tensor_tensor(gg[:], mod[:, 16:24], gpost[:], ALU.mult), reads=[r_small, r_mod], writes=[r_mod])
        s.do("dve", lambda e: e.tensor_copy(shift_bf[:], mod[:, 0:8]), reads=[r_mod], writes=[r_mod])
        s.barrier()

    bank = [0]

    nb_mod = [8]

    def nextbank():
        b = bank[0] % nb_mod[0]
        bank[0] = (b + 1) % nb_mod[0]
        return b

    def rsqrt_mean(ps_ap, out_ap, n, r_ps, r_out, tmp_ap, r_tmp):
        s.do("act", lambda e: e.activation(out=tmp_ap, in_=ps_ap, func=AF.Sqrt, bias=EPS_AP[:, 0:1], scale=1.0 / n), reads=[r_ps], writes=[r_tmp])
        s.do("dve", lambda e: e.reciprocal(out_ap, tmp_ap), reads=[r_tmp], writes=[r_out])


    if upto == 0:
        es_glob.close()
        raise _Stop(nc)
    with ExitStack() as es:
        xt = [sb(es, "xt%d" % i, [128, 8, TB], F32) for i in range(2)]
        r_xt = [Res("xt%d" % i) for i in range(2)]
        sq = sb(es, "sq", [128, 8, TB], BF16)
        r_sq = Res("sq")
        hT = [sb(es, "hT%d" % i, [128, 8, TB], BF16) for i in range(2)]
        r_hT = [Res("hT%d" % i) for i in range(2)]
        rstd = sb(es, "rstd", [128, TB], F32)
        r_rstd = Res("rstd")
        rtmp = sb(es, "rtmp", [128, TB], F32)
        r_rtmp = Res("rtmp")
        ost = [sb(es, "ost%d" % i, [128, TB], BF16) for i in range(4)]
        r_ost = [Res("ost%d" % i) for i in range(4)]
        noc = (INW + 127) // 128

        def mmsw(pe):
            inst = None
            for oc in range(noc):
                m = min(128, INW - oc * 128)
                for kc in range(8):
                    inst = pe.matmul(PS[1][0:m, oc:oc + 1], wi[:, kc, oc * 128:oc * 128 + m], shift_bf[:, kc:kc + 1],
                                     start=(kc == 0), stop=(kc == 7))
            return inst
        s.do("dve", lambda e: e.memset(swb[:], 0.0), writes=[r_mod])
        s.do("pe", mmsw, reads=r_wi + [r_mod], writes=[PSR[1]])
        s.do("dve", lambda e: e.tensor_copy(swb[:, 0:noc - 1], PS[1][:, 0:noc - 1]), reads=[PSR[1]], writes=[r_mod])
        s.do("dve", lambda e: e.tensor_copy(swb[0:32, noc - 1:noc], PS[1][0:32, noc - 1:noc]), reads=[PSR[1]], writes=[r_mod])

        xTv = xT.rearrange("(kc p) t -> p kc t", p=128)
        oi = 0
        def load1(tbb):
            ld("sp", xt[tbb % 2][:], xTv[:, :, tbb * TB:(tbb + 1) * TB], [r_xt[tbb % 2]])
        load1(0)
        for tb in range(NTB):
            t0 = tb * TB
            xb = xt[tb % 2]
            rx = r_xt[tb % 2]
            if tb + 1 < NTB:
                load1(tb + 1)
            def stats1(tbb):
                xb_ = xt[tbb % 2]
                rx_ = r_xt[tbb % 2]
                s.do("act", lambda e: e.activation(out=sq[:], in_=xb_[:], func=AF.Square), reads=[rx_], writes=[r_sq])
                b_ = nextbank()

                def mmss(pe):
                    inst = None
                    for kc in range(8):
                        inst = pe.matmul(PS[b_][:], ones_bf[:], sq[:, kc, :], start=(kc == 0), stop=(kc == 7))
                    return inst
                s.do("pe", mmss, reads=[r_sq, r_const], writes=[PSR[b_]])
                rsqrt_mean(PS[b_][:], rstd[:], D, PSR[b_], r_rstd, rtmp[:], r_rtmp)
                h_ = hT[tbb % 2]
                rh_ = r_hT[tbb % 2]
                for kc in range(8):
                    s.do("dve", lambda e, kc=kc: e.scalar_tensor_tensor(h_[:, kc, :], xb_[:, kc, :], gm[:, kc:kc + 1], rstd[:], ALU.mult, ALU.mult),
                         reads=[rx_, r_rstd, r_mod], writes=[rh_])
            if tb == 0:
                stats1(0)
            h = hT[tb % 2]
            rh = r_hT[tb % 2]
            for oc in range(noc):
                if oc == 24 and tb + 1 < NTB:
                    stats1(tb + 1)
                m = min(128, INW - oc * 128)
                b = nextbank()

                def mmp(pe, b=b, oc=oc, m=m):
                    inst = None
                    for kc in range(8):
                        inst = pe.matmul(PS[b][0:m, :], wi[:, kc, oc * 128:oc * 128 + m], h[:, kc, :], start=(kc == 0), stop=(kc == 7))
                    return inst
                s.do("pe", mmp, reads=r_wi + [rh], writes=[PSR[b]])
                o = ost[oi % 4]
                ro = r_ost[oi % 4]
                oi += 1
                segs = []
                p0 = 0
                while p0 < m:
                    r = oc * 128 + p0
                    if OFF_ZS <= r < OFF_Q or OFF_ZM <= r < OFF_GLS:
                        fn = AF.Silu
                    elif r >= OFF_GLS:
                        fn = AF.Sigmoid
                    else:
                        fn = None
                    p1 = p0 + 32
                    segs.append([p0, p1, fn])
                    p0 = p1
                merged = []
                for sg in segs:
                    if merged and merged[-1][2] == sg[2]:
                        merged[-1][1] = sg[1]
                    else:
                        merged.append(sg)
                legal = []
                for (a0, a1, fn) in merged:
                    while a0 < a1:
                        ln = min({0: 128, 32: 32, 64: 64, 96: 32}[a0], a1 - a0)
                        legal.append((a0, a0 + ln, fn))
                        a0 += ln
                for (q0, q1, fn) in legal:
                    if fn is None and oc % 2 == 1:
                        s.do("dve", lambda e, b=b, q0=q0, q1=q1, oc=oc, o=o: e.tensor_scalar(o[q0:q1, :], PS[b][q0:q1, :], swb[q0:q1, oc:oc + 1], 0.0, ALU.add, ALU.add),
                             reads=[PSR[b], r_mod], writes=[ro])
                    else:
                        f = AF.Identity if fn is None else fn
                        s.do("act", lambda e, b=b, q0=q0, q1=q1, oc=oc, o=o, f=f: e.activation(out=o[q0:q1, :], in_=PS[b][q0:q1, :], func=f, bias=swb[q0:q1, oc:oc + 1], scale=1.0),
                             reads=[PSR[b], r_mod], writes=[ro])
                s.dma("sp", lambda e, o=o, m=m, oc=oc: e.dma_start(out=projT[oc * 128:oc * 128 + m, t0:t0 + TB], in_=o[0:m, :]), reads=[ro])
        s.barrier()
    es_wi.close()

    if upto == 1:
        es_glob.close()
        raise _Stop(nc)
    nb_mod[0] = 6
    with ExitStack() as es:
        def ct(name):
            return sb(es, name, [128, 64], F32)
        r_co = Res("coef")
        lr_t, li_t, dt_t = ct("lr_t"), ct("li_t"), ct("dt_t")
        ld("sp", lr_t[:], lrT, [r_co])
        ld("sp", li_t[:], liT, [r_co])
        ld("sp", dt_t[:], ldt, [r_co])
        dsk = sb(es, "dsk", [128, 8], F32)
        ld("sp", dsk[:], dT, [r_co])
        tA, tB, tC, tD = ct("tA"), ct("tB"), ct("tC"), ct("tD")
        tI = sb(es, "tI", [128, 64], I32)

        def V(fn):
            s.do("dve", fn, reads=[r_co], writes=[r_co])

        def A(fn):
            s.do("act", fn, reads=[r_co], writes=[r_co])

        def sin_of(out_t, ang_t):
            V(lambda e: e.tensor_scalar(tA[:], ang_t[:], 1.0 / TWO_PI, 0.0, ALU.mult, ALU.add))
            V(lambda e: e.tensor_copy(tI[:], tA[:]))
            V(lambda e: e.tensor_copy(tA[:], tI[:]))
            V(lambda e: e.scalar_tensor_tensor(out_t[:], tA[:], -C1, ang_t[:], ALU.mult, ALU.add))
            V(lambda e: e.scalar_tensor_tensor(out_t[:], tA[:], -C2, out_t[:], ALU.mult, ALU.add))
            V(lambda e: e.tensor_scalar(out_t[:], out_t[:], SIN_CLAMP, -SIN_CLAMP, ALU.min, ALU.max))
            A(lambda e: e.activation(out=out_t[:], in_=out_t[:], func=AF.Sin))

        A(lambda e: e.activation(out=dt_t[:], in_=dt_t[:], func=AF.Exp))
        mag, ang = ct("mag"), ct("ang")
        V(lambda e: e.tensor_tensor(mag[:], lr_t[:], dt_t[:], ALU.mult))
        A(lambda e: e.activation(out=mag[:], in_=mag[:], func=AF.Exp))
        V(lambda e: e.tensor_tensor(ang[:], li_t[:], dt_t[:], ALU.mult))
        ER, EI = {}, {}
        ER[0], EI[0], ER[1], EI[1] = ct("er0"), ct("ei0"), ct("er1"), ct("ei1")
        V(lambda e: e.memset(ER[0][:], 1.0))
        V(lambda e: e.memset(EI[0][:], 0.0))
        sin_of(EI[1], ang)
        V(lambda e: e.tensor_scalar(tB[:], ang[:], math.pi / 2, 0.0, ALU.add, ALU.add))
        sin_of(ER[1], tB)
        V(lambda e: e.tensor_tensor(ER[1][:], ER[1][:], mag[:], ALU.mult))
        V(lambda e: e.tensor_tensor(EI[1][:], EI[1][:], mag[:], ALU.mult))

        def cmul(orr, oii, ar, ai, br, bi):
            V(lambda e: e.tensor_tensor(tC[:], ai[:], bi[:], ALU.mult))
            V(lambda e: e.tensor_tensor(tD[:], ai[:], br[:], ALU.mult))
            V(lambda e: e.tensor_tensor(tA[:], ar[:], br[:], ALU.mult))
            V(lambda e: e.tensor_tensor(tB[:], ar[:], bi[:], ALU.mult))
            V(lambda e: e.tensor_tensor(orr[:], tA[:], tC[:], ALU.subtract))
            V(lambda e: e.tensor_tensor(oii[:], tB[:], tD[:], ALU.add))

        for k in range(2, 9):
            ER[k], EI[k] = ct("er%d" % k), ct("ei%d" % k)
            cmul(ER[k], EI[k], ER[k - 1], EI[k - 1], ER[1], EI[1])
        k = 8
        while k < 2048:
            ER[2 * k], EI[2 * k] = ct("er%d" % (2 * k)), ct("ei%d" % (2 * k))
            cmul(ER[2 * k], EI[2 * k], ER[k], EI[k], ER[k], EI[k])
            k *= 2
        den, nr, fr, fi = ct("den"), ct("nr"), ct("fr"), ct("fi")
        V(lambda e: e.tensor_tensor(den[:], lr_t[:], lr_t[:], ALU.mult))
        V(lambda e: e.tensor_tensor(tA[:], li_t[:], li_t[:], ALU.mult))
        V(lambda e: e.tensor_tensor(den[:], den[:], tA[:], ALU.add))
        V(lambda e: e.reciprocal(den[:], den[:]))
        V(lambda e: e.tensor_scalar(nr[:], ER[1][:], -1.0, 0.0, ALU.add, ALU.add))
        V(lambda e: e.tensor_tensor(tA[:], nr[:], lr_t[:], ALU.mult))
        V(lambda e: e.tensor_tensor(tB[:], EI[1][:], li_t[:], ALU.mult))
        V(lambda e: e.tensor_tensor(fr[:], tA[:], tB[:], ALU.add))
        V(lambda e: e.tensor_tensor(fr[:], fr[:], den[:], ALU.mult))
        V(lambda e: e.tensor_tensor(tA[:], EI[1][:], lr_t[:], ALU.mult))
        V(lambda e: e.tensor_tensor(tB[:], nr[:], li_t[:], ALU.mult))
        V(lambda e: e.tensor_tensor(fi[:], tA[:], tB[:], ALU.subtract))
        V(lambda e: e.tensor_tensor(fi[:], fi[:], den[:], ALU.mult))
        GR, GIs = {}, {}
        for tau in range(8):
            GR[tau], GIs[tau] = ct("gr%d" % tau), ct("gi%d" % tau)
            cmul(GR[tau], GIs[tau], ER[tau], EI[tau], fr, fi)
            V(lambda e, tau=tau: e.tensor_scalar(GIs[tau][:], GIs[tau][:], sgn[:, 0:1], 0.0, ALU.mult, ALU.add))
        ERs, EIn, EIs = {}, {}, {}
        for k in range(0, 9):
            ERs[k], EIn[k] = ct("ers%d" % k), ct("ein%d" % k)
            V(lambda e, k=k: e.tensor_scalar(ERs[k][:], ER[k][:], sgn[:, 0:1], 0.0, ALU.mult, ALU.add))
            V(lambda e, k=k: e.tensor_scalar(EIn[k][:], EI[k][:], -1.0, 0.0, ALU.mult, ALU.add))
        LEV = [8 * 2 ** k for k in range(9)]
        for p in LEV:
            EIs[p] = ct("eis%d" % p)
            V(lambda e, p=p: e.tensor_scalar(EIs[p][:], EI[p][:], sgn[:, 0:1], 0.0, ALU.mult, ALU.add))

        chk(2.1)
        GRall = sb(es, "GRall", [128, 64, 8], F32)
        GIall = sb(es, "GIall", [128, 64, 8], F32)
        O1all = sb(es, "O1all", [128, 64, 9], F32)
        O2all = sb(es, "O2all", [128, 64, 9], F32)
        M1all = sb(es, "M1all", [128, 64, 9], F32)
        M2all = sb(es, "M2all", [128, 64, 9], F32)
        for tau in range(8):
            V(lambda e, tau=tau: e.tensor_copy(GRall[:, :, tau], GR[tau][:]))
            V(lambda e, tau=tau: e.tensor_copy(GIall[:, :, tau], GIs[tau][:]))
        for k in range(9):
            V(lambda e, k=k: e.tensor_copy(O1all[:, :, k], ERs[k][:]))
            V(lambda e, k=k: e.tensor_copy(O2all[:, :, k], EIn[k][:]))
            V(lambda e, k=k: e.tensor_copy(M1all[:, :, k], ER[LEV[k]][:]))
            V(lambda e, k=k: e.tensor_copy(M2all[:, :, k], EIs[LEV[k]][:]))
        bp = [sb(es, "bp%d" % i, [128, 128], BF16) for i in range(4)]
        cp = [sb(es, "cp%d" % i, [128, 128], BF16) for i in range(4)]
        r_bp = [Res() for _ in range(4)]
        r_cp = [Res() for _ in range(4)]
        Dall = [sb(es, "Dall%d" % i, [128, 8, 128], BF16) for i in range(4)]
        D2all = [sb(es, "D2all%d" % i, [128, 9, 128], BF16) for i in range(4)]
        Dtmp = sb(es, "Dtmp", [128, 9, 128], F32)
        Dtmpb = sb(es, "Dtmpb", [128, 9, 128], BF16)
        r_Dtmpb = Res()
        Dtmpb2 = sb(es, "Dtmpb2", [128, 9, 128], BF16)
        r_Dtmpb2 = Res()
        r_Dall = [Res() for _ in range(4)]
        r_D2all = [Res() for _ in range(4)]
        r_Dtmp = Res()
        LAg = [sb(es, "LAg%d" % i, [128, 8, 128], BF16) for i in range(4)]
        r_LAg = [Res() for _ in range(4)]
        MTg = [sb(es, "MTg%d" % i, [128, 9, 128], F32) for i in range(6)]
        r_MTg = [Res() for _ in range(6)]
        AB = sb(es, "AB", [128, 8, 128], BF16)
        r_AB = Res()
        OCm1 = sb(es, "OCm1", [128, 128], BF16)
        r_OCm1 = Res()
        OCt = [sb(es, "OCt%d" % i, [128, 8, 8, 128], BF16) for i in range(2)]
        r_OCt = [[Res() for _ in range(8)] for _ in range(2)]
        Tm = [sb(es, "Tm%d" % i, [128, 8, 128], BF16) for i in range(2)]
        r_Tm = [Res() for _ in range(2)]
        uT = [sb(es, "uT%d" % i, [128, S], BF16) for i in range(2)]
        r_uT = [Res() for _ in range(2)]
        yT = [sb(es, "yT%d" % i, [128, S], BF16) for i in range(1)] * 2
        r_yT = [Res()] * 2
        S32g = [sb(es, "S32g%d" % i, [128, 512], F32) for i in range(2)]
        r_S32g = [Res() for _ in range(2)]
        Sb = [sb(es, "Sb%d" % i, [128, 8, 512], BF16) for i in range(2)]
        r_Sb = [[Res() for _ in range(8)] for _ in range(2)]
        identB8 = ident[:].unsqueeze(1).broadcast_to([128, 8, 128])
        swapB8 = swapP[:].unsqueeze(1).broadcast_to([128, 8, 128])
        identB9 = ident[:].unsqueeze(1).broadcast_to([128, 9, 128])
        swapB9 = swapP[:].unsqueeze(1).broadcast_to([128, 9, 128])

        def bc(t, g, n):
            return t[:, g, :].unsqueeze(2).broadcast_to([128, n, 128])

        def A_build(g):
            p = g % 4
            p3 = g % 6
            rc = [r_co, r_const]
            ops = []

            def o0():
                ld("pool", bp[p][:], Bpad[g], [r_bp[p]])
                ld("pool", cp[p][:], Cpad[g], [r_cp[p]])
                s.do("dve", lambda e: e.tensor_tensor(Dall[p][:], identB8, bc(GRall, g, 8), ALU.mult), reads=rc, writes=[r_Dall[p]])
            ops.append(o0)
            ops.append(lambda: s.do("pool", lambda e: e.tensor_tensor(Dtmpb[:, 0:8, :], swapB8, bc(GIall, g, 8), ALU.mult), reads=rc, writes=[r_Dtmpb]))
            ops.append(lambda: s.do("dve", lambda e: e.tensor_tensor(Dall[p][:], Dall[p][:], Dtmpb[:, 0:8, :], ALU.add), reads=[r_Dtmpb, r_Dall[p]], writes=[r_Dall[p]]))
            ops.append(lambda: s.do("dve", lambda e: e.tensor_tensor(D2all[p][:], identB9, bc(O1all, g, 9), ALU.mult), reads=rc, writes=[r_D2all[p]]))
            ops.append(lambda: s.do("pool", lambda e: e.tensor_tensor(Dtmpb2[:], swapB9, bc(O2all, g, 9), ALU.mult), reads=rc, writes=[r_Dtmpb2]))
            ops.append(lambda: s.do("dve", lambda e: e.tensor_tensor(D2all[p][:], D2all[p][:], Dtmpb2[:], ALU.add), reads=[r_Dtmpb2, r_D2all[p]], writes=[r_D2all[p]]))
            ops.append(lambda: s.do("dve", lambda e: e.tensor_tensor(MTg[p3][:], identB9, bc(M1all, g, 9), ALU.mult), reads=rc, writes=[r_MTg[p3]]))
            ops.append(lambda: s.do("pool", lambda e: e.tensor_tensor(Dtmp[:], swapB9, bc(M2all, g, 9), ALU.mult), reads=rc, writes=[r_Dtmp]))
            ops.append(lambda: s.do("dve", lambda e: e.tensor_tensor(MTg[p3][:], MTg[p3][:], Dtmp[:], ALU.add), reads=[r_Dtmp, r_MTg[p3]], writes=[r_MTg[p3]]))
            return ops

        def A_units(g):
            p = g % 4
            ti, gl = divmod(g, 8)
            tp = ti % 2
            units = []

            def uLA(half):
                b = 2 + half
                s.do("pe", lambda pe: pe.matmul(PS[b][:], bp[p][:], Dall[p][:, half * 4:half * 4 + 4, :].rearrange("p a b -> p (a b)"), start=True, stop=True),
                     reads=[r_bp[p], r_Dall[p]], writes=[PSR[b]])
                s.do("act", lambda e: e.copy(LAg[p][:, half * 4:half * 4 + 4, :].rearrange("p a b -> p (a b)"), PS[b][:]), reads=[PSR[b]], writes=[r_LAg[p]])

            def uAB(half):
                b = 4 + half

                def mm(pe):
                    inst = None
                    for t4 in range(4):
                        inst = pe.matmul(PS[b][:, t4 * 128:(t4 + 1) * 128], Dall[p][:, half * 4 + t4, :], bp[p][:], start=True, stop=True)
                    return inst
                s.do("pe", mm, reads=[r_bp[p], r_Dall[p]], writes=[PSR[b]])
                s.do("act", lambda e: e.copy(AB[:, half * 4:half * 4 + 4, :].rearrange("p a b -> p (a b)"), PS[b][:]), reads=[PSR[b]], writes=[r_AB])

            def uOCm1():
                b = 2
                s.do("pe", lambda pe: pe.matmul(PS[b][:, 0:128], D2all[p][:, 0, :], cp[p][:], start=True, stop=True), reads=[r_cp[p], r_D2all[p]], writes=[PSR[b]])
                s.do("act", lambda e: e.copy(OCm1[:], PS[b][:, 0:128]), reads=[PSR[b]], writes=[r_OCm1])

            def uOC(half):
                b = 3 + 2 * half

                def mm(pe):
                    inst = None
                    for t4 in range(4):
                        inst = pe.matmul(PS[b][:, t4 * 128:(t4 + 1) * 128], D2all[p][:, 1 + half * 4 + t4, :], cp[p][:], start=True, stop=True)
                    return inst
                s.do("pe", mm, reads=[r_cp[p], r_D2all[p]], writes=[PSR[b]])
                s.do("act", lambda e: e.copy(OCt[tp][:, gl, half * 4:half * 4 + 4, :].rearrange("p a b -> p (a b)"), PS[b][:]), reads=[PSR[b]], writes=[r_OCt[tp][gl]])

            def uT_(half):
                bT = 6 + half

                def mm(pe):
                    inst = None
                    for t4 in range(4):
                        tau = half * 4 + t4
                        inst = pe.matmul(PS[bT][:, t4 * 128:(t4 + 1) * 128], AB[:, tau, :], OCm1[:], start=(gl == 0 and t4 == 0), stop=(gl == 7), skip_group_check=True)
                    return inst
                s.do("pe", mm, reads=[r_AB, r_OCm1], writes=[PSR[bT]])

            units = [lambda: uLA(0), lambda: uLA(1), lambda: uAB(0), lambda: uAB(1), uOCm1, lambda: uOC(0), lambda: uOC(1),
                     lambda: uT_(0), lambda: uT_(1)]
            return units

        def T_evac(ti):
            tp = ti % 2
            s.do("dve", lambda e: e.scalar_tensor_tensor(Tm[tp][:, 0, :], ident[:], dsk[:, ti:ti + 1], PS[6][:, 0:128], ALU.mult, ALU.add),
                 reads=[PSR[6], r_co, r_const], writes=[r_Tm[tp]])
            s.do("dve", lambda e: e.tensor_copy(Tm[tp][:, 1:4, :].rearrange("p a b -> p (a b)"), PS[6][:, 128:512]), reads=[PSR[6]], writes=[r_Tm[tp]])
            s.do("act", lambda e: e.copy(Tm[tp][:, 4:8, :].rearrange("p a b -> p (a b)"), PS[7][:]), reads=[PSR[7]], writes=[r_Tm[tp]])

        def load_u(ti):
            tp = ti % 2
            ld("sp", uT[tp][:], projT[OFF_U + ti * 128:OFF_U + (ti + 1) * 128, :], [r_uT[tp]])

        def output_stage(ti):
            tp = ti % 2
            u3 = uT[tp][:].rearrange("p (c j) -> p c j", j=8)
            y3 = yT[tp][:].rearrange("p (c j) -> p c j", j=8)
            for i in range(8):
                b = 2 + (i % 4)

                def mmY(pe, b=b, i=i):
                    inst = None
                    for j in range(i + 1):
                        inst = pe.matmul(PS[b][:], Tm[tp][:, i - j, :], u3[:, :, j], start=(j == 0), stop=(j == i))
                    for gl in range(8):
                        inst = pe.matmul(PS[b][:, 1:512], OCt[tp][:, gl, i, :], Sb[tp][:, gl, 0:511], start=False, stop=(gl == 7), skip_group_check=True)
                    return inst
                s.do("pe", mmY, reads=[r_Tm[tp], r_uT[tp]] + r_OCt[tp] + r_Sb[tp], writes=[PSR[b]])
                s.do("act", lambda e, b=b, i=i: e.activation(out=y3[:, :, i], in_=PS[b][:], func=AF.Gelu), reads=[PSR[b]], writes=[r_yT[tp]])
            s.dma("sp", lambda e: e.dma_start(out=gyT[ti * 128:(ti + 1) * 128, :], in_=yT[tp][:]), reads=[r_yT[tp]])

        load_u(0)
        for gg_ in range(4):
            for o in A_build(gg_):
                o()
        for gg_ in range(2):
            for u in A_units(gg_):
                u()
        for P in range(32):
            gs = (2 * P, 2 * P + 1)
            ti = gs[0] // 8
            tp = ti % 2
            if gs[0] % 8 == 0 and ti + 1 < 8:
                load_u(ti + 1)
            nxt = (A_units(gs[0] + 2) + A_units(gs[1] + 2)) if P + 1 < 32 else []
            bld = (A_build(gs[0] + 4) + A_build(gs[1] + 4)) if P + 2 < 32 else []
            u3 = uT[tp][:].rearrange("p (c j) -> p c j", j=8)
            for g in gs:
                b = g % 2
                p4 = g % 4

                def mmZ(pe, b=b, p4=p4, u3=u3):
                    inst = None
                    for j in range(8):
                        inst = pe.matmul(PS[b][:], LAg[p4][:, 7 - j, :], u3[:, :, j], start=(j == 0), stop=(j == 7))
                    return inst
                s.do("pe", mmZ, reads=[r_LAg[p4], r_uT[tp]], writes=[PSR[b]])
                s.do("act", lambda e, b=b: e.copy(S32g[b][:], PS[b][:]), reads=[PSR[b]], writes=[r_S32g[b]])
            for k in range(9):
                sh = 2 ** k
                for g in gs:
                    b = g % 2
                    p6 = g % 6
                    s.do("pe", lambda pe, b=b, p6=p6, k=k, sh=sh: pe.matmul(PS[b][:, sh:512], MTg[p6][:, k, :], S32g[b][:, 0:512 - sh], start=True, stop=True),
                         reads=[r_MTg[p6], r_S32g[b]], writes=[PSR[b]])
                for g in gs:
                    b = g % 2
                    s.do("dve", lambda e, b=b, sh=sh: e.tensor_tensor(S32g[b][:, sh:512], S32g[b][:, sh:512], PS[b][:, sh:512], ALU.add),
                         reads=[PSR[b], r_S32g[b]], writes=[r_S32g[b]])
                for q in (2 * k, 2 * k + 1):
                    if q < len(nxt):
                        nxt[q]()
                    if q < len(bld):
                        bld[q]()
            for g in gs:
                b = g % 2
                gl = g % 8
                s.do("act", lambda e, b=b, tp=tp, gl=gl: e.copy(Sb[tp][:, gl, :], S32g[b][:]), reads=[r_S32g[b]], writes=[r_Sb[tp][gl]])
            if P % 4 == 2:
                T_evac(ti)
            if P % 4 == 3:
                output_stage(ti)
        s.barrier()


    def kcview(ap_rows):
        return ap_rows.rearrange("(kc p) t -> p kc t", p=128)

    nb_mod[0] = 8
    if upto == 2:
        es_glob.close()
        raise _Stop(nc)
    with ExitStack() as es:
        wg = sb(es, "wg", [128, 8, 2 * D], BF16)
        wbs = sb(es, "wbs", [128, 8, D], BF16)
        r_wg = [Res() for _ in range(8)]
        r_wbs = [Res() for _ in range(8)]
        for kc in range(8):
            ld("pool", wg[:, kc, :], w_glu[kc * 128:(kc + 1) * 128, :], [r_wg[kc]])
        for kc in range(8):
            ld("pool", wbs[:, kc, :], w_brs[kc * 128:(kc + 1) * 128, :], [r_wbs[kc]])
        bglu = sb(es, "bglu", [128, 16], F32)
        r_bglu = Res()
        ld("sp", bglu[:], b_gluT, [r_bglu])
        gy = [sb(es, "gy%d" % i, [128, 8, TB], BF16) for i in range(2)]
        zs = [sb(es, "zs%d" % i, [128, 8, TB], BF16) for i in range(2)]
        gls = [sb(es, "gls%d" % i, [128, 8, TB], BF16) for i in range(2)]
        r_gy = [Res() for _ in range(2)]
        r_zs = [Res() for _ in range(2)]
        r_gls = [Res() for _ in range(2)]
        g1 = sb(es, "g1", [128, 8, TB], BF16)
        r_g1 = Res()
        mst = [sb(es, "mst%d" % i, [128, 8, TB], BF16) for i in range(2)]
        r_mst = [Res() for _ in range(2)]
        f1 = [sb(es, "f1_%d" % i, [128, TB], F32) for i in range(2)]
        f2 = [sb(es, "f2_%d" % i, [128, TB], F32) for i in range(2)]
        f3 = [sb(es, "f3_%d" % i, [128, TB], F32) for i in range(2)]
        r_f1 = [Res() for _ in range(2)]
        r_f2 = [Res() for _ in range(2)]
        r_f3 = [Res() for _ in range(2)]
        it = 0
        def load3(tbb):
            pp = tbb % 2
            c0_ = tbb * TB
            ld("sp", gy[pp][:], kcview(gyT)[:, :, c0_:c0_ + TB], [r_gy[pp]])
            ld("sp", zs[pp][:], kcview(projT[OFF_ZS:OFF_ZS + D, :])[:, :, c0_:c0_ + TB], [r_zs[pp]])
            ld("sp", gls[pp][:], kcview(projT[OFF_GLS:OFF_GLS + D, :])[:, :, c0_:c0_ + TB], [r_gls[pp]])
        load3(0)
        for tb in range(NTB):
            t0 = tb * TB
            p = tb % 2
            if tb + 1 < NTB:
                load3(tb + 1)
            for oc in range(8):
                ba = nextbank()
                bb = nextbank()
                q = it % 2
                it += 1

                def mmg(pe, ba=ba, bb=bb, oc=oc, p=p):
                    inst = None
                    for kc in range(8):
                        inst = pe.matmul(PS[ba][:], wg[:, kc, oc * 128:(oc + 1) * 128], gy[p][:, kc, :], start=(kc == 0), stop=(kc == 7))
                    for kc in range(8):
                        inst = pe.matmul(PS[bb][:], wg[:, kc, D + oc * 128:D + (oc + 1) * 128], gy[p][:, kc, :], start=(kc == 0), stop=(kc == 7))
                    return inst
                s.do("pe", mmg, reads=r_wg + [r_gy[p]], writes=[PSR[ba], PSR[bb]])
                s.do("act", lambda e, bb=bb, oc=oc, q=q: e.activation(out=f1[q][:], in_=PS[bb][:], func=AF.Sigmoid, bias=bglu[:, 8 + oc:9 + oc], scale=1.0),
                     reads=[PSR[bb], r_bglu], writes=[r_f1[q]])
                s.do("dve", lambda e, ba=ba, oc=oc, q=q: e.scalar_tensor_tensor(f2[q][:], PS[ba][:], bglu[:, oc:oc + 1], f1[q][:], ALU.add, ALU.mult),
                     reads=[PSR[ba], r_bglu, r_f1[q]], writes=[r_f2[q]])
                s.do("dve", lambda e, oc=oc, q=q, p=p: e.tensor_tensor(g1[:, oc, :], f2[q][:], zs[p][:, oc, :], ALU.mult), reads=[r_zs[p], r_f2[q]], writes=[r_g1])
            for oc in range(8):
                b = nextbank()
                q = it % 2
                it += 1

                def mmb(pe, b=b, oc=oc):
                    inst = None
                    for kc in range(8):
                        inst = pe.matmul(PS[b][:], wbs[:, kc, oc * 128:(oc + 1) * 128], g1[:, kc, :], start=(kc == 0), stop=(kc == 7))
                    return inst
                s.do("pe", mmb, reads=r_wbs + [r_g1], writes=[PSR[b]])
                s.do("dve", lambda e, b=b, oc=oc, q=q, p=p: e.tensor_tensor(mst[p][:, oc, :], PS[b][:], gls[p][:, oc, :], ALU.mult), reads=[PSR[b], r_gls[p]], writes=[r_mst[p]])
            s.dma("sp", lambda e, p=p, t0=t0: e.dma_start(out=kcview(msT)[:, :, t0:t0 + TB], in_=mst[p][:]), reads=[r_mst[p]])
        s.barrier()

    if upto == 3:
        es_glob.close()
        raise _Stop(nc)
    nb_mod[0] = 6
    with ExitStack() as es:
        wq = sb(es, "wq", [128, 2, 1536], BF16)
        wkv = sb(es, "wkv", [128, 2, 2048], BF16)
        r_w = Res()
        r_w2 = Res()
        Rm = sb(es, "Rm", [128, 96], BF16)
        invf = sb(es, "invf", [128, 1], F32)
        gq = sb(es, "gq", [128, 2], F32)
        gkv = sb(es, "gkv", [128, 2], F32)
        r_sm = [Res() for _ in range(6)]
        ld("pool", wq[:, 0, :], w_q[0:128, :], [r_w])
        ld("pool", wq[:, 1, :], w_q[128:256, :], [r_w2])
        ld("pool", wkv[:, 0, :], w_kv[0:128, :], [r_sm[0]])
        ld("pool", wkv[:, 1, :], w_kv[128:256, :], [r_sm[1]])
        ld("pool", Rm[:], cR, [r_sm[2]])
        ld("sp", invf[:], cInvf, [r_sm[3]])
        ld("sp", gq[:], g_qT, [r_sm[4]])
        ld("sp", gkv[:], g_kvT, [r_sm[5]])
        r_wall = [r_w, r_w2] + r_sm
        qn = sb(es, "qn", [128, 2, S], BF16)
        kvn = sb(es, "kvn", [128, 2, S], BF16)
        r_qn = [Res() for _ in range(NTB)]
        r_kvn = [Res() for _ in range(NTB)]
        cosT = sb(es, "cosT", [96, S], F32)
        sinT = sb(es, "sinT", [96, S], F32)
        r_trig = Res()
        KR = sb(es, "KR", [96, S], BF16)
        r_KR = Res()
        with ExitStack() as es2:
            posI = sb(es2, "posI", [96, S], I32)
            angT = sb(es2, "angT", [96, S], F32)
            tmpT = sb(es2, "tmpT", [96, S], F32)
            tmpI = sb(es2, "tmpI", [96, S], I32)
            r_tr = Res()
            ld("sp", posI[64:96, :], pos32, [r_tr])
            R = slice(64, 96)

            def V2(fn):
                s.do("dve", fn, reads=[r_tr, r_sm[3]], writes=[r_tr])

            V2(lambda e: e.tensor_copy(angT[R, :], posI[R, :]))
            V2(lambda e: e.tensor_scalar(angT[R, :], angT[R, :], invf[R, 0:1], 0.0, ALU.mult, ALU.add))

            def sin_big(out_t, ang_t):
                V2(lambda e: e.tensor_scalar(tmpT[R, :], ang_t[R, :], 1.0 / TWO_PI, 0.0, ALU.mult, ALU.add))
                V2(lambda e: e.tensor_copy(tmpI[R, :], tmpT[R, :]))
                V2(lambda e: e.tensor_copy(tmpT[R, :], tmpI[R, :]))
                V2(lambda e: e.scalar_tensor_tensor(out_t[R, :], tmpT[R, :], -C1, ang_t[R, :], ALU.mult, ALU.add))
                V2(lambda e: e.scalar_tensor_tensor(out_t[R, :], tmpT[R, :], -C2, out_t[R, :], ALU.mult, ALU.add))
                V2(lambda e: e.tensor_scalar(out_t[R, :], out_t[R, :], SIN_CLAMP, -SIN_CLAMP, ALU.min, ALU.max))
                s.do("act", lambda e: e.activation(out=out_t[R, :], in_=out_t[R, :], func=AF.Sin), reads=[r_tr], writes=[r_tr, r_trig])
            sin_big(sinT, angT)
            V2(lambda e: e.tensor_scalar(angT[R, :], angT[R, :], math.pi / 2, 0.0, ALU.add, ALU.add))
            sin_big(cosT, angT)
            s.barrier()
        ta = sb(es, "ta", [96, TB], F32)
        tb_ = sb(es, "tb_", [96, TB], F32)
        r_ta = Res()
        r_tb = Res()
        R = slice(64, 96)
        with ExitStack() as es3:
            krT = sb(es3, "krT", [96, S], BF16)
            r_krT = Res()
            ld("sp", krT[64:96, :], projT[OFF_KR:OFF_KR + 32, :], [r_krT])
            ql = [sb(es3, "ql%d" % i, [128, 2, TB], BF16) for i in range(2)]
            r_ql = [Res() for _ in range(2)]
            sq2 = sb(es3, "sq2", [128, 2, TB], BF16)
            r_sq2 = Res()
            rs2 = sb(es3, "rs2", [128, TB], F32)
            r_rs2 = Res()
            rt2 = sb(es3, "rt2", [128, TB], F32)
            r_rt2 = Res()
            it = 0
            for (off, dst, rdst, gvec, gi) in ((OFF_Q, qn, r_qn, gq, 4), (OFF_KV, kvn, r_kvn, gkv, 5)):
                for tb in range(NTB):
                    t0 = tb * TB
                    p = it % 2
                    it += 1
                    ld("sp", ql[p][:], kcview(projT[off:off + 256, :])[:, :, t0:t0 + TB], [r_ql[p]])
                    s.do("act", lambda e, p=p: e.activation(out=sq2[:], in_=ql[p][:], func=AF.Square), reads=[r_ql[p]], writes=[r_sq2])
                    b = nextbank()

                    def mmss2(pe, b=b):
                        inst = None
                        for kc in range(2):
                            inst = pe.matmul(PS[b][:], ones_bf[:], sq2[:, kc, :], start=(kc == 0), stop=(kc == 1))
                        return inst
                    s.do("pe", mmss2, reads=[r_sq2, r_const], writes=[PSR[b]])
                    rsqrt_mean(PS[b][:], rs2[:], 256, PSR[b], r_rs2, rt2[:], r_rt2)
                    for kc in range(2):
                        s.do("dve", lambda e, kc=kc, p=p, dst=dst, gvec=gvec, t0=t0: e.scalar_tensor_tensor(dst[:, kc, t0:t0 + TB], ql[p][:, kc, :], gvec[:, kc:kc + 1], rs2[:], ALU.mult, ALU.mult),
                             reads=[r_ql[p], r_rs2, r_sm[gi]], writes=[rdst[tb]])
            for tb in range(NTB):
                t0 = tb * TB
                s.do("pe", lambda pe, t0=t0: pe.matmul(PS[7][0:96, :], Rm[R, 0:96], krT[R, t0:t0 + TB], start=True, stop=True), reads=[r_krT, r_sm[2]], writes=[PSR[7]])
                s.do("dve", lambda e, t0=t0: e.tensor_tensor(ta[R, :], PS[7][R, :], sinT[R, t0:t0 + TB], ALU.mult), reads=[PSR[7], r_trig], writes=[r_ta])
                s.do("dve", lambda e, t0=t0: e.tensor_tensor(tb_[R, :], krT[R, t0:t0 + TB], cosT[R, t0:t0 + TB], ALU.mult), reads=[r_krT, r_trig], writes=[r_tb])
                s.do("dve", lambda e, t0=t0: e.tensor_tensor(KR[R, t0:t0 + TB], ta[R, :], tb_[R, :], ALU.add), reads=[r_ta, r_tb], writes=[r_KR])

            s.barrier()
        KT = sb(es, "KT", [96, 4, S], BF16)
        r_KT = [Res() for _ in range(4)]
        Vt = sb(es, "Vt", [128, 32, 4, 128], BF16)
        r_V = Res()
        qsb = [sb(es, "qsb%d" % i, [96, TB], BF16) for i in range(2)]
        r_qsb = [Res() for _ in range(2)]
        PT = [sb(es, "PT%d" % i, [128, TB], BF16) for i in range(4)]
        r_PT = [Res() for _ in range(4)]
        recs = sb(es, "recs", [128, TB], F32)
        r_recs = Res()
        r_ones = Res()
        s.do("dve", lambda e: e.memset(Vt[:, :, :, 64:128], 1.0), writes=[r_ones])
        zmb = [sb(es, "zmb%d" % i, [64, TB], BF16) for i in range(3)]
        r_zmb = [Res() for _ in range(3)]
        ymst = [sb(es, "ymst%d" % i, [64, TB], BF16) for i in range(2)]
        r_ymst = [Res() for _ in range(2)]
        fa = sb(es, "fa", [64, TB], F32)
        fb = sb(es, "fb", [64, TB], F32)
        fc = sb(es, "fc", [64, TB], F32)
        r_fa, r_fb, r_fc = Res(), Res(), Res()
        wkv_h = wkv[:].rearrange("p k (h c) -> p k h c", c=128)
        ipt = 0
        ihq = 0
        for hg in range(4):
            for hl in range(4):
                h = hg * 4 + hl
                for tb in range(NTB):
                    t0 = tb * TB
                    b = nextbank()

                    def mmk(pe, b=b, h=h, t0=t0):
                        inst = None
                        for kc in range(2):
                            inst = pe.matmul(PS[b][0:64, :], wkv[:, kc, h * 128:h * 128 + 64], kvn[:, kc, t0:t0 + TB], start=(kc == 0), stop=(kc == 1))
                        return inst
                    s.do("pe", mmk, reads=[r_sm[0], r_sm[1], r_kvn[tb]], writes=[PSR[b]])
                    if tb % 2 == 0:
                        s.do("act", lambda e, b=b, hl=hl, t0=t0: e.copy(KT[0:64, hl, t0:t0 + TB], PS[b][0:64, :]), reads=[PSR[b]], writes=[r_KT[hl]])
                    else:
                        s.do("dve", lambda e, b=b, hl=hl, t0=t0: e.tensor_copy(KT[0:64, hl, t0:t0 + TB], PS[b][0:64, :]), reads=[PSR[b]], writes=[r_KT[hl]])
                s.do("pool", lambda e, hl=hl: e.tensor_copy(KT[R, hl, :], KR[R, :]), reads=[r_KR], writes=[r_KT[hl]])
            for kt in range(32):
                b = nextbank()

                def mmv(pe, b=b, kt=kt, hg=hg):
                    inst = None
                    for kc in range(2):
                        inst = pe.matmul(PS[b][:, 0:256].rearrange("p (h c) -> p h c", c=64), kvn[:, kc, kt * 128:(kt + 1) * 128],
                                         wkv_h[:, kc, hg * 4:hg * 4 + 4, 64:128], start=(kc == 0), stop=(kc == 1))
                    return inst
                s.do("pe", mmv, reads=[r_sm[0], r_sm[1], r_kvn[kt // 4]], writes=[PSR[b]])
                if kt % 2 == 0:
                    s.do("act", lambda e, b=b, kt=kt: e.copy(Vt[:, kt, :, 0:64], PS[b][:, 0:256].rearrange("p (h c) -> p h c", c=64)), reads=[PSR[b]], writes=[r_V])
                else:
                    s.do("dve", lambda e, b=b, kt=kt: e.tensor_copy(Vt[:, kt, :, 0:64], PS[b][:, 0:256].rearrange("p (h c) -> p h c", c=64)), reads=[PSR[b]], writes=[r_V])
            items = [(qb, hl) for qb in (7, 0, 6, 1, 5, 2, 4, 3) for hl in range(4)]
            meta = {}

            def prepA(ii):
                qb, hl = items[ii]
                h = hg * 4 + hl
                t0 = qb * TB
                pq = (ihq0 + ii) % 2
                Q = qsb[pq]
                rQ = r_qsb[pq]
                par = (ihq0 + ii) % 2
                pz = (ihq0 + ii) % 3
                meta[ii] = dict(qb=qb, hl=hl, h=h, t0=t0, pq=pq, pz=pz, Q=Q, rQ=rQ, bo=3 + par, nkt=4 * qb + 4)
                ld("sp", zmb[pz][:], projT[OFF_ZM + h * 64:OFF_ZM + (h + 1) * 64, t0:t0 + TB], [r_zmb[pz]])

                def mmq(pe):
                    inst = None
                    for kc in range(2):
                        inst = pe.matmul(PS[5][0:96, :], wq[:, kc, h * 96:(h + 1) * 96], qn[:, kc, t0:t0 + TB], start=(kc == 0), stop=(kc == 1))
                    return inst
                s.do("pe", mmq, reads=[r_w, r_w2, r_qn[qb]], writes=[PSR[5]])
                s.do("act", lambda e: e.copy(Q[:], PS[5][0:96, :]), reads=[PSR[5]], writes=[rQ])

            def prepB(ii):
                m = meta[ii]
                Q, rQ, t0 = m["Q"], m["rQ"], m["t0"]
                s.do("pe", lambda pe: pe.matmul(PS[6][0:96, :], Rm[R, 0:96], Q[R, :], start=True, stop=True), reads=[rQ, r_sm[2]], writes=[PSR[6]])
                s.do("dve", lambda e: e.tensor_tensor(ta[R, :], PS[6][R, :], sinT[R, t0:t0 + TB], ALU.mult), reads=[PSR[6], r_trig], writes=[r_ta])
                s.do("dve", lambda e: e.tensor_tensor(tb_[R, :], PS[5][R, :], cosT[R, t0:t0 + TB], ALU.mult), reads=[PSR[5], r_trig], writes=[r_tb])
                s.do("pool", lambda e: e.tensor_tensor(Q[R, :], ta[R, :], tb_[R, :], ALU.add), reads=[r_ta, r_tb], writes=[rQ])

            flat = []
            for ii, (qb, hl) in enumerate(items):
                for kt in range(4 * qb + 4):
                    flat.append((ii, kt))
            NF = len(flat)
            ptof = {}

            def issueS(n):
                ii, kt = flat[n]
                m = meta[ii]
                a = kt - 4 * m["qb"]
                c0 = 128 * a if a >= 0 else 0
                bs = n % 3
                Q, hl = m["Q"], m["hl"]
                s.do("pe", lambda pe: pe.matmul(PS[bs][:, c0:TB], KT[0:96, hl, kt * 128:(kt + 1) * 128], Q[0:96, c0:TB], start=True, stop=True),
                     reads=[r_KT[hl], m["rQ"]], writes=[PSR[bs]])

            def issueExp(n):
                ii, kt = flat[n]
                m = meta[ii]
                a = kt - 4 * m["qb"]
                c0 = 128 * a if a >= 0 else 0
                bs = n % 3
                pt = n % 4
                s.do("act", lambda e: e.activation(out=PT[pt][:, c0:TB], in_=PS[bs][:, c0:TB], func=AF.Exp, scale=ATT_SCALE),
                     reads=[PSR[bs]], writes=[r_PT[pt]])
                if a >= 0:
                    s.do("pool", lambda e: e.memset(PT[pt][64:128, c0:c0 + 64], 0.0), reads=[r_PT[pt]], writes=[r_PT[pt]])

            def issuePV(n):
                ii, kt = flat[n]
                m = meta[ii]
                a = kt - 4 * m["qb"]
                c0 = 128 * a if a >= 0 else 0
                pt = n % 4
                bo, nkt, hl = m["bo"], m["nkt"], m["hl"]
                def mmpv(pe):
                    if SPLIT_PV:
                        pe.matmul(PS[bo][0:64, c0:TB], Vt[:, kt, hl, 0:64], PT[pt][:, c0:TB], start=(kt == 0), stop=(kt == nkt - 1), skip_group_check=True)
                        return pe.matmul(PS[bo][64:128, c0:TB], Vt[:, kt, hl, 64:128], PT[pt][:, c0:TB], start=(kt == 0), stop=(kt == nkt - 1), skip_group_check=True)
                    return pe.matmul(PS[bo][:, c0:TB], Vt[:, kt, hl, :], PT[pt][:, c0:TB], start=(kt == 0), stop=(kt == nkt - 1), skip_group_check=True)
                s.do("pe", mmpv, reads=[r_V, r_ones, r_PT[pt]], writes=[PSR[bo]])

            def finalizeA(ii):
                m = meta[ii]
                bo = m["bo"]
                s.do("dve", lambda e: e.reciprocal(recs[64:128, :], PS[bo][64:128, :]), reads=[PSR[bo]], writes=[r_recs])

            def finalizeB(ii):
                m = meta[ii]
                pq, pz, bo, h, t0 = m["pq"], m["pz"], m["bo"], m["h"], m["t0"]
                zm = zmb[pz]
                s.do("pe", lambda pe: pe.matmul(PS[7][0:64, :], ident[64:128, 64:128], recs[64:128, :], start=True, stop=True), reads=[r_recs, r_const], writes=[PSR[7]])
                s.do("act", lambda e: e.copy(fa[:], PS[7][0:64, :]), reads=[PSR[7]], writes=[r_fa])
                s.do("pool", lambda e: e.tensor_tensor(fc[:], fa[:], zm[:], ALU.mult), reads=[r_fa, r_zmb[pz]], writes=[r_fc])
                s.do("dve", lambda e: e.tensor_tensor(ymst[pq][:], PS[bo][0:64, :], fc[:], ALU.mult), reads=[PSR[bo], r_fc], writes=[r_ymst[pq]])
                s.dma("sp", lambda e: e.dma_start(out=ymT[h * 64:(h + 1) * 64, t0:t0 + TB], in_=ymst[pq][:]), reads=[r_ymst[pq]])

            ihq0 = hg * len(items)
            prepA(0)
            prepB(0)
            issueS(0)
            issueS(1)
            for n in range(NF):
                ii, kt = flat[n]
                if kt == 0 and ii + 1 < len(items):
                    prepA(ii + 1)
                if kt == 1 and ii + 1 < len(items):
                    prepB(ii + 1)
                if kt == (3 if meta[ii]["nkt"] >= 8 else 1) and ii >= 1:
                    finalizeB(ii - 1)
                if n + 2 < NF:
                    issueS(n + 2)
                issueExp(n)
                issuePV(n)
                if FILLER_N > 0:
                    s.do("pe", lambda pe: pe.matmul(PS[7][:, 0:FILLER_N], ones_bf[:], ones_bf[:].unsqueeze(1).broadcast_to([128, FILLER_N // 128, 128]).rearrange("p a b -> p (a b)") if False else PT[0][:, 0:FILLER_N], start=True, stop=True))
                if kt == meta[ii]["nkt"] - 1:
                    finalizeA(ii)
            finalizeB(len(items) - 1)
        s.barrier()

    nb_mod[0] = 8
    if upto == 4:
        es_glob.close()
        raise _Stop(nc)
    with ExitStack() as es:
        wbm = sb(es, "wbm", [128, 8, D], BF16)
        wo = sb(es, "wo", [128, 8, D], BF16)
        r_wbm = [Res() for _ in range(8)]
        r_wo = [Res() for _ in range(8)]
        for kc in range(8):
            ld("pool", wbm[:, kc, :], w_brm[kc * 128:(kc + 1) * 128, :], [r_wbm[kc]])
        for kc in range(8):
            ld("pool", wo[:, kc, :], w_o[kc * 128:(kc + 1) * 128, :], [r_wo[kc]])
        ym = [sb(es, "ym%d" % i, [128, 8, TB], BF16) for i in range(2)]
        glm = [sb(es, "glm%d" % i, [128, 8, TB], BF16) for i in range(2)]
        msb = [sb(es, "msb%d" % i, [128, 8, TB], BF16) for i in range(2)]
        xb2 = [sb(es, "xb2_%d" % i, [128, 8, TB], F32) for i in range(2)]
        r_ym = [Res() for _ in range(2)]
        r_glm = [Res() for _ in range(2)]
        r_msb = [Res() for _ in range(2)]
        r_xb2 = [Res() for _ in range(2)]
        mg = sb(es, "mg", [128, 8, TB], BF16)
        r_mg = Res()
        o32 = sb(es, "o32", [128, 8, TB], F32)
        r_o32 = Res()
        sq6 = sb(es, "sq6", [128, 8, TB], BF16)
        r_sq6 = Res()
        h1 = [sb(es, "h1_%d" % i, [128, TB], F32) for i in range(2)]
        h2 = [sb(es, "h2_%d" % i, [128, TB], F32) for i in range(2)]
        r_h1 = [Res() for _ in range(2)]
        r_h2 = [Res() for _ in range(2)]
        rs6 = sb(es, "rs6", [128, TB], F32)
        rt6 = sb(es, "rt6", [128, TB], F32)
        r_rs6, r_rt6 = Res(), Res()
        it = 0
        def load6(tbb):
            pp = tbb % 2
            c0_ = tbb * TB
            ld("sp", ym[pp][:], kcview(ymT)[:, :, c0_:c0_ + TB], [r_ym[pp]])
            ld("sp", glm[pp][:], kcview(projT[OFF_GLM:OFF_GLM + D, :])[:, :, c0_:c0_ + TB], [r_glm[pp]])
            ld("sp", msb[pp][:], kcview(msT)[:, :, c0_:c0_ + TB], [r_msb[pp]])
            ld("sp", xb2[pp][:], kcview(xT)[:, :, c0_:c0_ + TB], [r_xb2[pp]])
        load6(0)
        for tb in range(NTB):
            t0 = tb * TB
            p = tb % 2
            if tb + 1 < NTB:
                load6(tb + 1)
            for oc in range(8):
                b = nextbank()
                q = it % 2
                it += 1

                def mm1(pe, b=b, oc=oc, p=p):
                    inst = None
                    for kc in range(8):
                        inst = pe.matmul(PS[b][:], wbm[:, kc, oc * 128:(oc + 1) * 128], ym[p][:, kc, :], start=(kc == 0), stop=(kc == 7))
                    return inst
                s.do("pe", mm1, reads=r_wbm + [r_ym[p]], writes=[PSR[b]])
                s.do("dve", lambda e, b=b, q=q, oc=oc, p=p: e.tensor_tensor(h2[q][:], PS[b][:], glm[p][:, oc, :], ALU.mult), reads=[PSR[b], r_glm[p]], writes=[r_h2[q]])
                s.do("pool", lambda e, oc=oc, q=q, p=p: e.tensor_tensor(mg[:, oc, :], h2[q][:], msb[p][:, oc, :], ALU.add), reads=[r_h2[q], r_msb[p]], writes=[r_mg])
            for oc in range(8):
                b = nextbank()

                def mm2(pe, b=b, oc=oc):
                    inst = None
                    for kc in range(8):
                        inst = pe.matmul(PS[b][:], wo[:, kc, oc * 128:(oc + 1) * 128], mg[:, kc, :], start=(kc == 0), stop=(kc == 7))
                    return inst
                s.do("pe", mm2, reads=r_wo + [r_mg], writes=[PSR[b]])
                s.do("act", lambda e, b=b, oc=oc: e.copy(o32[:, oc, :], PS[b][:]), reads=[PSR[b]], writes=[r_o32])
                s.do("act", lambda e, b=b, oc=oc: e.activation(out=sq6[:, oc, :], in_=PS[b][:], func=AF.Square), reads=[PSR[b]], writes=[r_sq6])
            b = nextbank()

            def mmss6(pe, b=b):
                inst = None
                for kc in range(8):
                    inst = pe.matmul(PS[b][:], ones_bf[:], sq6[:, kc, :], start=(kc == 0), stop=(kc == 7))
                return inst
            s.do("pe", mmss6, reads=[r_sq6, r_const], writes=[PSR[b]])
            rsqrt_mean(PS[b][:], rs6[:], D, PSR[b], r_rs6, rt6[:], r_rt6)
            for oc in range(8):
                s.do("dve", lambda e, oc=oc: e.scalar_tensor_tensor(o32[:, oc, :], o32[:, oc, :], gg[:, oc:oc + 1], rs6[:], ALU.mult, ALU.mult),
                     reads=[r_o32, r_rs6, r_mod], writes=[r_o32])
                s.do("dve", lambda e, oc=oc, p=p: e.tensor_tensor(xb2[p][:, oc, :], o32[:, oc, :], xb2[p][:, oc, :], ALU.add),
                     reads=[r_o32, r_xb2[p]], writes=[r_xb2[p]])
            s.dma("sp", lambda e, p=p, t0=t0: e.dma_start(out=kcview(outT)[:, :, t0:t0 + TB], in_=xb2[p][:]), reads=[r_xb2[p]])
        s.barrier()
    es_glob.close()
    return nc


def _prep_shared(inp):
    f = np.float32
    sh = {}
    sh["w_ada"] = np.ascontiguousarray(inp["w_ada"][0], f)
    sh["b_adaT"] = np.ascontiguousarray(inp["b_ada"][0].reshape(24, 128).T, f)
    sh["g_preT"] = np.ascontiguousarray(inp["g_pre"][0].reshape(8, 128).T, f)
    sh["w_in"] = np.ascontiguousarray(inp["w_in"][0], f)
    a_re = np.asarray(inp["ssm_a_re"][0], f)
    a_im = np.asarray(inp["ssm_a_im"][0], f)
    sh["lrT"] = np.ascontiguousarray(np.concatenate([a_re.T, a_re.T], 0), f)
    sh["liT"] = np.ascontiguousarray(np.concatenate([a_im.T, a_im.T], 0), f)
    sh["ldt"] = np.ascontiguousarray(np.tile(np.asarray(inp["ssm_log_dt"][0], f)[None, :], (128, 1)), f)
    b_re = np.asarray(inp["ssm_b_re"][0], f)
    b_im = np.asarray(inp["ssm_b_im"][0], f)
    c_re = np.asarray(inp["ssm_c_re"][0], f)
    c_im = np.asarray(inp["ssm_c_im"][0], f)
    Bpad = np.zeros((64, 128, 128), f)
    Cpad = np.zeros((64, 128, 128), f)
    for g in range(64):
        gl = g % 8
        Bpad[g, 0:64, 16 * gl:16 * gl + 16] = b_re[g]
        Bpad[g, 64:128, 16 * gl:16 * gl + 16] = b_im[g]
        Cpad[g, 0:64, 16 * gl:16 * gl + 16] = c_re[g].T
        Cpad[g, 64:128, 16 * gl:16 * gl + 16] = c_im[g].T
    sh["Bpad"] = Bpad
    sh["Cpad"] = Cpad
    sh["dT"] = np.ascontiguousarray(np.asarray(inp["ssm_d"][0], f).reshape(8, 128).T, f)
    sh["w_glu"] = np.ascontiguousarray(inp["w_glu"][0], f)
    sh["b_gluT"] = np.ascontiguousarray(inp["b_glu"][0].reshape(16, 128).T, f)
    sh["g_qT"] = np.ascontiguousarray(inp["g_q_norm"][0].reshape(2, 128).T, f)
    sh["w_q"] = np.ascontiguousarray(inp["w_q_up"][0], f)
    sh["g_kvT"] = np.ascontiguousarray(inp["g_kv_norm"][0].reshape(2, 128).T, f)
    sh["w_kv"] = np.ascontiguousarray(inp["w_kv_up"][0], f)
    sh["w_brs"] = np.ascontiguousarray(inp["w_br_ssm"][0], f)
    sh["w_brm"] = np.ascontiguousarray(inp["w_br_mla"][0], f)
    sh["w_o"] = np.ascontiguousarray(inp["w_out"][0], f)
    sh["g_postT"] = np.ascontiguousarray(inp["g_post"][0].reshape(8, 128).T, f)
    sh["cI"] = np.eye(128, dtype=f)
    P = np.zeros((128, 128), f)
    for n in range(64):
        P[n, 64 + n] = 1.0
        P[64 + n, n] = 1.0
    sh["cP"] = P
    sh["cSgn"] = np.concatenate([np.ones(64, f), -np.ones(64, f)])[:, None]
    Rb = np.zeros((128, 96), f)
    for i in range(16):
        Rb[64 + 16 + i, 64 + i] = -1.0
        Rb[64 + i, 64 + 16 + i] = 1.0
    sh["cR"] = Rb
    inv = (10000.0 ** (-np.arange(0, 32, 2, dtype=np.float32) / np.float32(32))).astype(f)
    iv = np.zeros((128, 1), f)
    iv[64:80, 0] = inv
    iv[80:96, 0] = inv
    sh["cInvf"] = iv
    return sh


_NC_CACHE = {}


def kernel(**inputs):
    inp = {k: np.asarray(v) for k, v in inputs.items()}
    sh = _prep_shared(inp)
    x = np.asarray(inp["x"], np.float32)
    c = np.asarray(inp["c"], np.float32)
    pos = np.asarray(inp["positions"], np.int32)
    in_maps = []
    for b in range(8):
        m = dict(sh)
        m["xT"] = np.ascontiguousarray(x[b].T)
        m["cT"] = np.ascontiguousarray(c[b].reshape(8, 128).T)
        m["pos32"] = np.ascontiguousarray(np.tile(pos[b][None, :], (32, 1)))
        in_maps.append(m)
    if "nc" not in _NC_CACHE:
        _NC_CACHE["nc"] = build(DEBUG_OUT)
    nc = _NC_CACHE["nc"]
    res = run_bass_kernel_spmd(nc, in_maps, core_ids=list(range(8)))
    _NC_CACHE["last"] = res
    out = np.stack([np.ascontiguousarray(res.results[b]["outT"].T) for b in range(8)], 0)
    return out.astype(np.float32)
```
